# Optimizing a Trainium2 kernel written in Bass

```python
import math
import jax, jax.numpy as jnp
from jax import lax
import numpy as np

D_MODEL = 2048
BATCH = 4
SEQ = 4096
DEPTH = 2

N_MIXERS = 2
HEAD_DIM = 128
N_MIX_HEADS = (3 * D_MODEL) // (4 * HEAD_DIM)
MIX_WIDTH = N_MIX_HEADS * HEAD_DIM
N_MEM_HEADS = 4
MEM_WIDTH = N_MEM_HEADS * HEAD_DIM
MEM_TOKENS = 256
IN_WIDTH = 3 * MIX_WIDTH + MEM_WIDTH
OUT_WIDTH = MIX_WIDTH + MEM_WIDTH
D_FF = 4 * D_MODEL
DIFF_HALF = HEAD_DIM // 2
DILATED_GROUPS = ((128, 1), (512, 4), (2048, 16))
HEADS_PER_GROUP = N_MIX_HEADS // len(DILATED_GROUPS)
QBLOCK = 128
NORM_EPS = 1e-6

kernel_name = "hybrid_diffattn_dilated_memory_block"


def _alibi_slopes(n):
    def pow2(m):
        start = 2.0 ** (-(2.0 ** -(math.log2(m) - 3)))
        return [start * start ** i for i in range(m)]
    def slopes(m):
        if math.log2(m).is_integer():
            return pow2(m)
        c = 2 ** math.floor(math.log2(m))
        return pow2(c) + slopes(2 * c)[0::2][: m - c]
    return jnp.asarray(sorted(slopes(n), reverse=True), dtype=jnp.float32)


def _rmsnorm(x, g):
    xf = x.astype(jnp.float32)
    y = xf * lax.rsqrt(jnp.mean(xf * xf, axis=-1, keepdims=True) + NORM_EPS)
    return (y * g.astype(jnp.float32)).astype(x.dtype)


def _heads(t, n_heads):
    b, s, _ = t.shape
    return t.reshape(b, s, n_heads, -1).transpose(0, 2, 1, 3)


def _diff_attention(q1, q2, k1, k2, v, lam, slopes):
    b, h, s, d = q1.shape
    scale = d ** -0.5
    kpos = jnp.arange(s)
    def block(i):
        q0 = i * QBLOCK
        qa = lax.dynamic_slice_in_dim(q1, q0, QBLOCK, axis=2)
        qb = lax.dynamic_slice_in_dim(q2, q0, QBLOCK, axis=2)
        dist = (q0 + jnp.arange(QBLOCK))[:, None] - kpos[None, :]
        bias = jnp.where(dist >= 0, -slopes[:, None, None] * dist.astype(jnp.float32), -jnp.inf)
        s1 = jnp.einsum('bhqd,bhkd->bhqk', qa, k1, preferred_element_type=jnp.float32) * scale + bias
        s2 = jnp.einsum('bhqd,bhkd->bhqk', qb, k2, preferred_element_type=jnp.float32) * scale + bias
        a = (jax.nn.softmax(s1, axis=-1) - lam * jax.nn.softmax(s2, axis=-1)).astype(v.dtype)
        return jnp.einsum('bhqk,bhkd->bhqd', a, v)
    out = lax.map(block, jnp.arange(s // QBLOCK))
    return out.transpose(1, 2, 0, 3, 4).reshape(b, h, s, v.shape[-1])


def _dilated_group(q, k, v, window, dilation, slopes):
    b, h, s, dh = q.shape
    n_keys = window // dilation + 1
    offs = jnp.arange(n_keys) * dilation
    scale = dh ** -0.5
    bias = -slopes[:, None, None] * offs.astype(jnp.float32)[None, None, :]
    def block(i):
        q0 = i * QBLOCK
        qb = lax.dynamic_slice_in_dim(q, q0, QBLOCK, axis=2)
        kidx = (q0 + jnp.arange(QBLOCK))[:, None] - offs[None, :]
        valid = kidx >= 0
        kidx = jnp.maximum(kidx, 0)
        kb = k[:, :, kidx]
        vb = v[:, :, kidx]
        sc = jnp.einsum('bhqd,bhqjd->bhqj', qb, kb, preferred_element_type=jnp.float32) * scale + bias
        sc = jnp.where(valid, sc, -jnp.inf)
        lse = jax.nn.logsumexp(sc, axis=-1)
        p = jnp.exp(sc - lse[..., None]).astype(v.dtype)
        return jnp.einsum('bhqj,bhqjd->bhqd', p, vb), lse
    o, lse = lax.map(block, jnp.arange(s // QBLOCK))
    o = o.transpose(1, 2, 0, 3, 4).reshape(b, h, s, dh)
    lse = lse.transpose(1, 2, 0, 3).reshape(b, h, s)
    return o, lse


def _dilated_attention(q, k, v, slopes):
    outs, lses = [], []
    for g, (window, dilation) in enumerate(DILATED_GROUPS):
        sl = slice(g * HEADS_PER_GROUP, (g + 1) * HEADS_PER_GROUP)
        o, lse = _dilated_group(q[:, sl], k[:, sl], v[:, sl], window, dilation, slopes[sl])
        outs.append(o)
        lses.append(lse)
    alpha = jax.nn.softmax(jnp.stack(lses, axis=0), axis=0)
    return jnp.concatenate([o * alpha[g][..., None].astype(o.dtype) for g, o in enumerate(outs)], axis=1)


def _memory_attention(qm, km, vm):
    sc = jnp.einsum('bhqd,bhmd->bhqm', qm, km, preferred_element_type=jnp.float32) * (qm.shape[-1] ** -0.5)
    p = jax.nn.softmax(sc, axis=-1).astype(vm.dtype)
    return jnp.einsum('bhqm,bhmd->bhqd', p, vm)


def setup_inputs(seed: int = 0) -> dict:
    key = jax.random.key(seed)
    ks = jax.random.split(key, 14)
    n_diff = (DEPTH + N_MIXERS - 1) // N_MIXERS
    f32 = jnp.float32
    def nrm(k, shape, scale):
        return jax.random.normal(k, shape, f32) * scale
    return {
        "x": nrm(ks[0], (BATCH, SEQ, D_MODEL), 1.0),
        "mem": nrm(ks[1], (BATCH, MEM_TOKENS, D_MODEL), 1.0),
        "g_attn": 1.0 + nrm(ks[2], (DEPTH, D_MODEL), 0.02),
        "w_in": nrm(ks[3], (DEPTH, D_MODEL, IN_WIDTH), D_MODEL ** -0.5),
        "w_out": nrm(ks[4], (DEPTH, OUT_WIDTH, D_MODEL), OUT_WIDTH ** -0.5),
        "lambda_qk": nrm(ks[5], (n_diff, 4, DIFF_HALF), 0.1),
        "diff_subln_g": 1.0 + nrm(ks[6], (n_diff, HEAD_DIM), 0.02),
        "g_mem": 1.0 + nrm(ks[7], (D_MODEL,), 0.02),
        "w_mem_kv": nrm(ks[8], (DEPTH, D_MODEL, 2 * MEM_WIDTH), D_MODEL ** -0.5),
        "g_mlp": 1.0 + nrm(ks[9], (DEPTH, D_MODEL), 0.02),
        "w_mlp1": nrm(ks[10], (DEPTH, D_MODEL, D_FF), D_MODEL ** -0.5),
        "w_mlp2": nrm(ks[11], (DEPTH, D_FF, D_MODEL), D_FF ** -0.5),
        "g_final": 1.0 + nrm(ks[12], (D_MODEL,), 0.02),
    }


def reference(x, mem, g_attn, w_in, w_out, lambda_qk, diff_subln_g, g_mem, w_mem_kv,
              g_mlp, w_mlp1, w_mlp2, g_final):
    b, s, d_model = x.shape
    slopes = _alibi_slopes(N_MIX_HEADS)
    memn = _rmsnorm(mem, g_mem)
    for i in range(DEPTH):
        h = _rmsnorm(x, g_attn[i])
        proj = h @ w_in[i]
        q, k, v, qm = jnp.split(proj, [MIX_WIDTH, 2 * MIX_WIDTH, 3 * MIX_WIDTH], axis=-1)
        q, k, v = _heads(q, N_MIX_HEADS), _heads(k, N_MIX_HEADS), _heads(v, N_MIX_HEADS)
        if i % N_MIXERS == 0:
            j = i // N_MIXERS
            lam_init = 0.8 - 0.6 * math.exp(-0.3 * i)
            lp = lambda_qk[j].astype(jnp.float32)
            lam = jnp.exp(jnp.sum(lp[0] * lp[1])) - jnp.exp(jnp.sum(lp[2] * lp[3])) + lam_init
            o = _diff_attention(q[..., :DIFF_HALF], q[..., DIFF_HALF:],
                                k[..., :DIFF_HALF], k[..., DIFF_HALF:], v, lam, slopes)
            o = _rmsnorm(o, diff_subln_g[j]) * (1.0 - lam_init)
        else:
            o = _dilated_attention(q, k, v, slopes)
        km, vm = jnp.split(memn @ w_mem_kv[i], 2, axis=-1)
        om = _memory_attention(_heads(qm, N_MEM_HEADS), _heads(km, N_MEM_HEADS), _heads(vm, N_MEM_HEADS))
        heads = jnp.concatenate([o, om], axis=1)
        merged = heads.transpose(0, 2, 1, 3).reshape(b, s, OUT_WIDTH)
        x = x + merged @ w_out[i]
        hm = _rmsnorm(x, g_mlp[i])
        x = x + jnp.square(jax.nn.relu(hm @ w_mlp1[i])) @ w_mlp2[i]
    return _rmsnorm(x, g_final)
```

```python
import math
from contextlib import ExitStack
import numpy as np
import ml_dtypes
import concourse.bass as bass
import concourse.mybir as mybir
from concourse.bass_utils import run_bass_kernel_spmd

F32 = mybir.dt.float32
BF16 = mybir.dt.bfloat16
AF = mybir.ActivationFunctionType
ALU = mybir.AluOpType

D = 2048
NCH = 16
TOK = 2048
TT = 1024
NT = TOK // TT
NH = 12
DFF = 8192
EPS = 1e-6
NEG = -30000.0
DIL = ((128, 1), (512, 4), (2048, 16))


def alibi_slopes(n):
    def pow2(m):
        start = 2.0 ** (-(2.0 ** -(math.log2(m) - 3)))
        return [start * start ** i for i in range(m)]

    def slopes(m):
        if math.log2(m).is_integer():
            return pow2(m)
        c = 2 ** math.floor(math.log2(m))
        return pow2(c) + slopes(2 * c)[0::2][: m - c]
    return np.asarray(sorted(slopes(n), reverse=True), dtype=np.float32)


SLOPES = alibi_slopes(NH)


class Tok:
    __slots__ = ("sem", "val")

    def __init__(self, sem, val):
        self.sem = sem
        self.val = val


class Eng:
    def __init__(self, P, name, attr):
        self.P = P
        self.name = name
        self.attr = attr
        self.sem = P.new_sem("e_" + name)
        self.cnt = 0
        self.waited = {}

    def wait_all(self, deps):
        waits = []
        for d in deps:
            k = id(d.sem)
            if self.waited.get(k, 0) >= d.val:
                continue
            self.waited[k] = d.val
            waits.append(d)

        def body(e):
            for w in waits:
                e.wait_ge(w.sem, w.val)
        if waits:
            getattr(self.P.block, self.attr)(body)

    def __call__(self, fn, deps=(), sig=True, dma_sem=None):
        waits = []
        for d in deps:
            if d is None:
                continue
            k = id(d.sem)
            if self.waited.get(k, 0) >= d.val:
                continue
            self.waited[k] = d.val
            waits.append(d)
        tok = None
        if dma_sem is not None:
            cnt = self.P.dma_cnt.get(id(dma_sem), (dma_sem, 0))[1] + 16
            self.P.dma_cnt[id(dma_sem)] = (dma_sem, cnt)
            tok = Tok(dma_sem, cnt)
        elif sig:
            self.cnt += 1
            tok = Tok(self.sem, self.cnt)

        def body(e):
            for w in waits:
                e.wait_ge(w.sem, w.val)
            ins = fn(e)
            if dma_sem is not None:
                ins.then_inc(dma_sem, 16)
            elif sig:
                ins.then_inc(self.sem, 1)
        getattr(self.P.block, self.attr)(body)
        return tok


class Prog:
    def __init__(self, nc, es, block):
        self.nc = nc
        self.es = es
        self.block = block
        self.dma_cnt = {}
        self.nsem = 0
        self.pe = Eng(self, "pe", "tensor")
        self.act = Eng(self, "act", "scalar")
        self.dve = Eng(self, "dve", "vector")
        self.pool = Eng(self, "pool", "gpsimd")
        self.sp = Eng(self, "sp", "sync")
        self.engs = [self.pe, self.act, self.dve, self.pool, self.sp]
        self.psum = es.enter_context(nc.psum_tensor("psum", [128, 8, 512], F32))
        self.ps_free = [None] * 8
        self.ps_i = 0
        self.NSLOT = 3
        self.wp = es.enter_context(nc.sbuf_tensor("wpanel", [128, self.NSLOT, 16, 256], BF16))
        self.wp_sem = [self.new_sem("wp%d" % i) for i in range(self.NSLOT)]
        self.wp_free = [None] * self.NSLOT
        self.wp_i = 0
        self.evq = 0

    def new_sem(self, name):
        self.nsem += 1
        return self.es.enter_context(self.nc.semaphore("%s_%d" % (name, self.nsem)))

    def sbuf(self, es, name, shape, dt):
        self.nsem += 1
        return es.enter_context(self.nc.sbuf_tensor("sb_%s_%d" % (name, self.nsem), shape, dt))

    def bank(self):
        i = self.ps_i
        self.ps_i = (i + 1) % 8
        return i, self.ps_free[i]

    def barrier(self):
        toks = [Tok(e.sem, e.cnt) for e in self.engs if e.cnt > 0]
        toks += [Tok(sem, cnt) for (sem, cnt) in self.dma_cnt.values()]
        for e in self.engs:
            e.wait_all(toks)
        self.ps_free = [None] * 8
        self.wp_free = [None] * self.NSLOT

    def load_panel(self, w2d, k0, c0, extra_deps=()):
        s = self.wp_i
        self.wp_i = (s + 1) % self.NSLOT
        src = w2d[k0:k0 + 2048, c0:c0 + 256].rearrange("(kc p) n -> p kc n", p=128)
        dst = self.wp[:, s]
        tok = self.pool(lambda e: e.dma_start(out=dst, in_=src), deps=[self.wp_free[s], *extra_deps],
                        dma_sem=self.wp_sem[s])
        return s, tok

    def ev_engine(self):
        self.evq += 1
        return self.act if (self.evq & 1) else self.dve


def norm_tile(P, es, x_dram, t0, g_sb, ones_bf, out_h, name, y_dram=None, ring=None):
    xs, xs_sem, xs_free, sq, sq_free, rstd, lnb = ring[:7]
    banks = []
    for st in range(TT // 512):
        banks.append(P.bank())
    last_mm = None
    for c in range(NCH):
        s = c % 2
        src = x_dram[c, :, t0:t0 + TT]
        ld = P.sp(lambda e, s=s, src=src: e.dma_start(out=xs[:, s], in_=src), deps=[xs_free[s]], dma_sem=xs_sem[s])
        t_sq = P.act(lambda e, s=s: e.activation(out=sq[:, s], in_=xs[:, s], func=AF.Square), deps=[ld, sq_free[s]])
        xs_free[s] = t_sq
        for st in range(TT // 512):
            b, bfree = banks[st]
            last_mm = P.pe(lambda e, b=b, s=s, st=st, c=c: e.matmul(P.psum[:, b, :], ones_bf[:, :], sq[:, s, st * 512:(st + 1) * 512],
                                                                     start=(c == 0), stop=(c == NCH - 1)),
                           deps=[t_sq, bfree if c == 0 else None], sig=(st == TT // 512 - 1))
        sq_free[s] = last_mm
    for st in range(TT // 512):
        b, _ = banks[st]
        t1 = P.act(lambda e, b=b, st=st: e.activation(out=rstd[:, st * 512:(st + 1) * 512], in_=P.psum[:, b, :], func=AF.Ln,
                                                      scale=1.0 / D, bias=lnb[:, 0:1]), deps=[last_mm, rstd_free(ring)])
        P.ps_free[b] = t1
        t2 = P.act(lambda e, st=st: e.activation(out=rstd[:, st * 512:(st + 1) * 512], in_=rstd[:, st * 512:(st + 1) * 512],
                                                 func=AF.Exp, scale=-0.5), deps=[t1])
    t_r = t2
    last = None
    for c in range(NCH):
        s = c % 2
        src = x_dram[c, :, t0:t0 + TT]
        ld = P.sp(lambda e, s=s, src=src: e.dma_start(out=xs[:, s], in_=src), deps=[xs_free[s]], dma_sem=xs_sem[s])
        if y_dram is None:
            t = P.dve(lambda e, s=s, c=c: e.scalar_tensor_tensor(out=out_h[:, c, :], in0=xs[:, s], scalar=g_sb[:, c:c + 1], in1=rstd[:, :],
                                                                 op0=ALU.mult, op1=ALU.mult), deps=[ld, t_r, ring_hfree(ring)])
            xs_free[s] = t
            last = t
        else:
            t = P.dve(lambda e, s=s, c=c: e.scalar_tensor_tensor(out=xs[:, s], in0=xs[:, s], scalar=g_sb[:, c:c + 1], in1=rstd[:, :],
                                                                 op0=ALU.mult, op1=ALU.mult), deps=[ld, t_r])
            dst = y_dram[c, :, t0:t0 + TT]
            st_tok = P.sp(lambda e, s=s, dst=dst: e.dma_start(out=dst, in_=xs[:, s]), deps=[t], dma_sem=ring[7][s])
            xs_free[s] = st_tok
            last = st_tok
    ring[8][0] = last
    return last


def rstd_free(ring):
    return ring[8][0]


def ring_hfree(ring):
    return ring[9][0]


def make_norm_ring(P, es, lnb):
    xs = P.sbuf(es, "nxs", [128, 2, TT], F32)
    sq = P.sbuf(es, "nsq", [128, 2, TT], BF16)
    rstd = P.sbuf(es, "nrstd", [128, TT], F32)
    xs_sem = [P.new_sem("nxs") for _ in range(2)]
    st_sem = [P.new_sem("nst") for _ in range(2)]
    return [xs, xs_sem, [None, None], sq, [None, None], rstd, lnb, st_sem, [None], [None]]


def linear_fm(P, w2d, ksegs, colpanels, rhs_fn, nst, evac_fn, w_deps=()):
    for cp in colpanels:
        banks = {}
        for ks, k0 in enumerate(ksegs):
            slot, ltok = P.load_panel(w2d, k0, cp * 256, extra_deps=w_deps)
            last = None
            for o in range(2):
                for st in range(nst):
                    if ks == 0:
                        banks[(o, st)] = P.bank()
                    b, bfree = banks[(o, st)]
                    for kc in range(16):
                        rhs, rdep = rhs_fn(ks, kc, st)
                        first = (ks == 0 and kc == 0)
                        lastmm = (ks == len(ksegs) - 1 and kc == 15)
                        last = P.pe(lambda e, b=b, slot=slot, kc=kc, o=o, rhs=rhs, first=first, lastmm=lastmm:
                                    e.matmul(P.psum[:, b, :], P.wp[:, slot, kc, o * 128:(o + 1) * 128], rhs, start=first, stop=lastmm),
                                    deps=[ltok, rdep, bfree if first else None], sig=(kc == 15))
                    if ks == len(ksegs) - 1:
                        P.ps_free[b] = evac_fn(cp, o, st, b, last)
            P.wp_free[slot] = last


def build(mode):
    nc = bass.Bass("TRN2", target_bir_lowering=False)
    es = ExitStack()

    def dram(name, shape, dt, kind):
        if kind == "in":
            return nc.dram_tensor(name, shape, dt, kind="ExternalInput").ap()
        if kind == "out":
            return nc.dram_tensor(name, shape, dt, kind="ExternalOutput").ap()
        return nc.dram_tensor(name, shape, dt).ap()

    fused = (mode == "FUSED")
    has = {"A0": mode in ("A0", "FUSED"), "B0": mode in ("B0A1", "FUSED"), "A1": mode in ("B0A1", "FUSED"),
           "B1": mode in ("B1", "FUSED")}
    io = {}
    io["xT"] = dram("xT", [NCH, 128, TOK], F32, "in")
    layers = [l for l in (0, 1) if has["A%d" % l] or has["B%d" % l]]
    W = {}
    for l in (0, 1):
        if has["A%d" % l]:
            W["w_in", l] = dram("w_in%d" % l, [D, 5120], F32, "in")
            W["g_attn", l] = dram("g_attn%d" % l, [128, NCH], F32, "in")
        if has["B%d" % l]:
            W["w_out", l] = dram("w_out%d" % l, [D, D], F32, "in")
            W["w_kv", l] = dram("w_kv%d" % l, [D, 1024], F32, "in")
            W["w1", l] = dram("w1_%d" % l, [D, DFF], F32, "in")
            W["w2", l] = dram("w2_%d" % l, [DFF, D], F32, "in")
            W["g_mlp", l] = dram("g_mlp%d" % l, [128, NCH], F32, "in")
    if has["B0"] or has["B1"]:
        io["memT"] = dram("memT", [NCH, 128, 256], F32, "in")
        io["g_mem"] = dram("g_mem", [128, NCH], F32, "in")
        io["prevmask"] = dram("prevmask", [128, 1], F32, "in")
    if has["B0"]:
        io["lamqk"] = dram("lamqk", [128, 256], F32, "in")
        io["gsub"] = dram("gsub", [128, 1], F32, "in")
        io["bias0"] = dram("bias0", [NH, 5, 128, 512], F32, "in")
    if has["B1"]:
        io["bias1"] = dram("bias1", [NH, 2, 128, 512], F32, "in")
        io["g_final"] = dram("g_final", [128, NCH], F32, "in")
        io["yT"] = dram("yT", [NCH, 128, TOK], F32, "out")

    def qkv(l, kind):
        return dict(qT=dram("qT%d" % l, [16, 128, TOK], BF16, kind), kT=dram("kT%d" % l, [NH, 128, TOK], BF16, kind),
                    vh=dram("vh%d" % l, [NH, TOK, 128], BF16, kind))
    QKV = {}
    PREV = {}
    for l in (0, 1):
        a, b = has["A%d" % l], has["B%d" % l]
        if a and b:
            QKV[l] = qkv(l, "scratch")
        elif a:
            QKV[l] = qkv(l, "out")
        elif b:
            QKV[l] = qkv(l, "in")
        if b:
            if fused:
                PREV[l] = None
            else:
                PREV[l] = dict(kT=dram("kTp%d" % l, [NH, 128, TOK], BF16, "in"), vh=dram("vhp%d" % l, [NH, TOK, 128], BF16, "in"))
    if has["B0"] or has["B1"]:
        io["mg"] = dram("mg", [16, 128, TOK], BF16, "scratch" if fused else "out")
    if has["B0"]:
        io["x1"] = dram("x1", [NCH, 128, TOK], F32, "scratch" if fused else "out")
    if mode == "B1":
        io["x2"] = dram("x2", [NCH, 128, TOK], F32, "scratch")

    block = es.enter_context(nc.Block())
    P = Prog(nc, es, block)

    ones_bf = P.sbuf(es, "ones_bf", [128, 128], BF16)
    lnb = P.sbuf(es, "lnb", [128, 1], F32)
    t_c1 = P.pool(lambda e: e.memset(ones_bf[:], 1.0))
    t_c2 = P.pool(lambda e: e.memset(lnb[:], EPS))
    csem = P.new_sem("const")

    def load_const(es_, name, src, shape, dt=F32):
        t = P.sbuf(es_, name, shape, dt)
        tok = P.sp(lambda e: e.dma_start(out=t[:], in_=src), dma_sem=csem)
        return t, tok

    P.barrier()

    def stage_A(l, x_dram):
        with ExitStack() as s:
            g_sb, tg = load_const(s, "gA", W["g_attn", l], [128, NCH])
            ring = make_norm_ring(P, s, lnb)
            hT = P.sbuf(s, "hT", [128, NCH, TT], BF16)
            stg = P.sbuf(s, "stgA", [128, 2, 2, TT], BF16)
            stg_sem = [P.new_sem("stgA") for _ in range(2)]
            stg_free = [None, None]
            vst = P.sbuf(s, "vstA", [128, 2, TT // 128, 256], BF16)
            vst_sem = [P.new_sem("vstA") for _ in range(2)]
            vst_free = [None, None]
            P.barrier()
            q = QKV[l]
            sc_mix = (64 ** -0.5) if l == 0 else (128 ** -0.5)
            sc_mem = 128 ** -0.5
            h_readers = [None]
            for t in range(NT):
                t0 = t * TT
                ring[9][0] = h_readers[0]
                tn = norm_tile(P, s, x_dram, t0, g_sb, ones_bf, hT, "A", ring=ring)
                cnt = [0]
                for cp in list(range(0, 12)) + [18, 19]:
                    si = cnt[0] % 2
                    cnt[0] += 1
                    evs = []

                    def evac(cp_, o, st, b, mm, si=si, evs=evs):
                        if cp_ < 6:
                            sc = sc_mix
                        elif cp_ < 12:
                            sc = 1.0
                        else:
                            sc = sc_mem
                        eng = P.ev_engine()
                        dst = stg[:, si, o, st * 512:(st + 1) * 512]
                        if eng is P.act:
                            tk = eng(lambda e: e.activation(out=dst, in_=P.psum[:, b, :], func=AF.Copy, scale=float(sc)),
                                     deps=[mm, stg_free[si]])
                        else:
                            tk = eng(lambda e: e.tensor_scalar(out=dst, in0=P.psum[:, b, :], scalar1=float(sc), scalar2=None, op0=ALU.mult),
                                     deps=[mm, stg_free[si]])
                        evs.append(tk)
                        return tk
                    linear_fm(P, W["w_in", l], [0], [cp], lambda ks, kc, st: (hT[:, kc, st * 512:(st + 1) * 512], tn), TT // 512, evac)
                    last = None
                    for o in range(2):
                        if cp < 6:
                            dst = q["qT"][cp * 2 + o, :, t0:t0 + TT]
                        elif cp < 12:
                            dst = q["kT"][(cp - 6) * 2 + o, :, t0:t0 + TT]
                        else:
                            dst = q["qT"][12 + (cp - 18) * 2 + o, :, t0:t0 + TT]
                        last = P.sp(lambda e, dst=dst, o=o, si=si: e.dma_start(out=dst, in_=stg[:, si, o, :]), deps=evs, dma_sem=stg_sem[si])
                    stg_free[si] = last
                for cp in range(12, 18):
                    si = cnt[0] % 2
                    cnt[0] += 1
                    slot, ltok = P.load_panel(W["w_in", l], 0, cp * 256)
                    evs = []
                    last = None
                    for bp in range(TT // 256):
                        b, bfree = P.bank()
                        for j in range(2):
                            blk = bp * 2 + j
                            for kc in range(16):
                                last = P.pe(lambda e, b=b, j=j, blk=blk, kc=kc, slot=slot:
                                            e.matmul(P.psum[:, b, j * 256:(j + 1) * 256], hT[:, kc, blk * 128:(blk + 1) * 128], P.wp[:, slot, kc, :],
                                                     start=(kc == 0), stop=(kc == 15)),
                                            deps=[ltok, tn, bfree if (kc == 0 and j == 0) else None], sig=(kc == 15))
                        eng = P.ev_engine()
                        dst = vst[:, si, bp * 2:bp * 2 + 2, :]
                        srcp = P.psum[:, b, :].rearrange("p (j c) -> p j c", j=2)
                        if eng is P.act:
                            tk = eng(lambda e, dst=dst, srcp=srcp: e.activation(out=dst, in_=srcp, func=AF.Copy), deps=[last, vst_free[si]])
                        else:
                            tk = eng(lambda e, dst=dst, srcp=srcp: e.tensor_copy(out=dst, in_=srcp), deps=[last, vst_free[si]])
                        P.ps_free[b] = tk
                        evs.append(tk)
                    P.wp_free[slot] = last
                    lst = None
                    for o in range(2):
                        hh = (cp - 12) * 2 + o
                        dst = q["vh"][hh, t0:t0 + TT, :].rearrange("(blk p) d -> p blk d", p=128)
                        lst = P.sp(lambda e, dst=dst, o=o, si=si: e.dma_start(out=dst, in_=vst[:, si, :, o * 128:(o + 1) * 128]),
                                   deps=evs, dma_sem=vst_sem[si])
                    vst_free[si] = lst
                    h_readers[0] = last
            P.barrier()


    def mem_norm(s):
        memn = P.sbuf(s, "memn", [128, NCH, 256], BF16)
        s = ExitStack()
        gm, _ = load_const(s, "gmem", io["g_mem"], [128, NCH])
        mraw = P.sbuf(s, "mraw", [128, NCH, 256], F32)
        msq = P.sbuf(s, "msq", [128, NCH, 256], BF16)
        mrs = P.sbuf(s, "mrs", [128, 256], F32)
        P.sp(lambda e: e.dma_start(out=mraw[:], in_=io["memT"].rearrange("c p m -> p c m")), dma_sem=csem)
        P.barrier()
        t1 = P.act(lambda e: e.activation(out=msq[:], in_=mraw[:], func=AF.Square))
        b, bf = P.bank()
        for c in range(NCH):
            mm = P.pe(lambda e: e.matmul(P.psum[:, b, 0:256], ones_bf[:, :], msq[:, c, :], start=(c == 0), stop=(c == NCH - 1)),
                      deps=[t1, bf], sig=(c == NCH - 1))
        t2 = P.act(lambda e: e.activation(out=mrs[:], in_=P.psum[:, b, 0:256], func=AF.Ln, scale=1.0 / D, bias=lnb[:, 0:1]), deps=[mm])
        P.ps_free[b] = t2
        t3 = P.act(lambda e: e.activation(out=mrs[:], in_=mrs[:], func=AF.Exp, scale=-0.5), deps=[t2])
        for c in range(NCH):
            t4 = P.dve(lambda e: e.scalar_tensor_tensor(out=memn[:, c, :], in0=mraw[:, c, :], scalar=gm[:, c:c + 1], in1=mrs[:, :],
                                                        op0=ALU.mult, op1=ALU.mult), deps=[t3])
        P.barrier()
        s.close()
        return memn, t4

    def mem_kv(l, memn, tmem, kmT, vm):
        last = None
        for cp in range(4):
            slot, ltok = P.load_panel(W["w_kv", l], 0, cp * 256)
            if cp < 2:
                for o in range(2):
                    b, bf = P.bank()
                    for kc in range(16):
                        mm = P.pe(lambda e: e.matmul(P.psum[:, b, 0:256], P.wp[:, slot, kc, o * 128:(o + 1) * 128], memn[:, kc, :],
                                                     start=(kc == 0), stop=(kc == 15)), deps=[ltok, tmem, bf if kc == 0 else None], sig=(kc == 15))
                    last = P.dve(lambda e: e.tensor_copy(out=kmT[:, cp * 2 + o, :], in_=P.psum[:, b, 0:256]), deps=[mm])
                    P.ps_free[b] = last
            else:
                for mb in range(2):
                    b, bf = P.bank()
                    for kc in range(16):
                        mm = P.pe(lambda e: e.matmul(P.psum[:, b, 0:256], memn[:, kc, mb * 128:(mb + 1) * 128], P.wp[:, slot, kc, :],
                                                     start=(kc == 0), stop=(kc == 15)), deps=[ltok, tmem, bf if kc == 0 else None], sig=(kc == 15))
                    last = P.dve(lambda e: e.tensor_copy(out=vm[:, mb, (cp - 2) * 256:(cp - 1) * 256], in_=P.psum[:, b, 0:256]), deps=[mm])
                    P.ps_free[b] = last
            P.wp_free[slot] = mm
        return last

    class AttnCtx:
        pass

    def attn_common(s):
        A = AttnCtx()
        A.KT = P.sbuf(s, "aKT", [128, 2, 2 * TOK], BF16)
        A.V = P.sbuf(s, "aV", [128, 2, 32, 128], BF16)
        A.q = P.sbuf(s, "aq", [128, 2, TOK], BF16)
        A.ld_sem = [P.new_sem("ald") for _ in range(2)]
        A.ld_free = [None, None]
        A.u = P.sbuf(s, "au", [128, 3, 512], F32)
        A.u_free = [None] * 3
        A.pm = P.sbuf(s, "apm", [128, 3, 512], BF16)
        A.pm_free = [None] * 3
        A.ui = 0
        A.rd = P.sbuf(s, "ard", [128, 2, 512], F32)
        A.rd_free = [None, None]
        A.rdi = 0
        A.stg = P.sbuf(s, "astg", [128, 2, 512], BF16)
        A.stg_sem = [P.new_sem("astg") for _ in range(2)]
        A.stg_free = [None, None]
        A.stgi = 0
        A.pmask, _ = load_const(s, "pmask", io["prevmask"], [128, 1])
        A.sb = 0
        A.ob = 0
        return A

    def s_bank(A):
        b = A.sb
        A.sb = (b + 1) % 4
        return b, P.ps_free[b]

    def od_banks(A):
        p = A.ob
        A.ob ^= 1
        return (4 + 2 * p, P.ps_free[4 + 2 * p]), (5 + 2 * p, P.ps_free[5 + 2 * p])

    def softmax_block(A, bS, mmS, bias_ap, scalar, c):
        i = A.ui
        A.ui = (i + 1) % 3
        if bias_ap is not None:
            t1 = P.dve(lambda e: e.scalar_tensor_tensor(out=A.u[:, i, :], in0=P.psum[:, bS, :], scalar=scalar, in1=bias_ap,
                                                        op0=ALU.add, op1=ALU.add), deps=[mmS, A.u_free[i]])
            P.ps_free[bS] = t1
            t2 = P.act(lambda e: e.activation(out=A.pm[:, i, :], in_=A.u[:, i, :], func=AF.Exp, bias=A.cbias(c)), deps=[t1, A.pm_free[i]])
            A.u_free[i] = t2
        else:
            t2 = P.act(lambda e: e.activation(out=A.pm[:, i, :], in_=P.psum[:, bS, :], func=AF.Exp), deps=[mmS, A.pm_free[i]])
            P.ps_free[bS] = t2
        return i, t2

    def recip_den(A, bD, mm_last):
        r = A.rdi
        A.rdi ^= 1
        t1 = P.act(lambda e: e.activation(out=A.rd[:, r, :], in_=P.psum[:, bD, :], func=AF.Ln), deps=[mm_last, A.rd_free[r]])
        P.ps_free[bD] = t1
        t2 = P.act(lambda e: e.activation(out=A.rd[:, r, :], in_=A.rd[:, r, :], func=AF.Exp, scale=-1.0), deps=[t1])
        return r, t2

    def store_mg(A, h, col0, producer):
        si = A.stgi
        A.stgi ^= 1
        tk = producer(A.stg[:, si, :], [A.stg_free[si]])
        dst = io["mg"][h, :, col0:col0 + 512]
        A.stg_free[si] = P.sp(lambda e: e.dma_start(out=dst, in_=A.stg[:, si, :]), deps=[tk], dma_sem=A.stg_sem[si])

    def load_head(A, l, h, slot, dil=None):
        qk = QKV[l]
        pv = PREV[l]
        fr = [A.ld_free[slot]]
        sem = A.ld_sem[slot]
        P.sp(lambda e: e.dma_start(out=A.KT[:, slot, 0:TOK], in_=pv["kT"][h]), deps=fr, dma_sem=sem)
        P.sp(lambda e: e.dma_start(out=A.KT[:, slot, TOK:2 * TOK], in_=qk["kT"][h]), deps=fr, dma_sem=sem)
        tk = None
        if dil is None or dil == 1:
            P.sp(lambda e: e.dma_start(out=A.V[:, slot, 0:16, :], in_=pv["vh"][h].rearrange("(b p) d -> p b d", p=128)), deps=fr, dma_sem=sem)
            P.sp(lambda e: e.dma_start(out=A.V[:, slot, 16:32, :], in_=qk["vh"][h].rearrange("(b p) d -> p b d", p=128)), deps=fr, dma_sem=sem)
        else:
            nU = 16 // dil
            for r in range(dil):
                for half, src in ((0, pv["vh"][h]), (1, qk["vh"][h])):
                    sv = src.rearrange("(U i r) d -> r i U d", i=128, r=dil)[r]
                    b0 = r * 2 * nU + half * nU
                    P.sp(lambda e: e.dma_start(out=A.V[:, slot, b0:b0 + nU, :], in_=sv), deps=fr, dma_sem=sem)
        tk = P.sp(lambda e: e.dma_start(out=A.q[:, slot, :], in_=qk["qT"][h]), deps=fr, dma_sem=sem)
        return tk

    def stage_ATT0(kmT, vm, t_kv):
        l = 0
        with ExitStack() as s:
            A = attn_common(s)
            TS = P.sbuf(s, "aTS", [128, 2, 5, 512], F32)
            ts_sem = [P.new_sem("ats") for _ in range(2)]
            lam_in, _ = load_const(s, "lamin", io["lamqk"], [128, 256])
            gsub, _ = load_const(s, "gsub", io["gsub"], [128, 1])
            lsc = P.sbuf(s, "lsc", [128, 8], F32)
            lpr = P.sbuf(s, "lpr", [128, 128], F32)
            om = P.sbuf(s, "aom", [128, 2, 512], F32)
            om_free = [None, None]
            osq = P.sbuf(s, "aosq", [128, 512], BF16)
            cb = P.sbuf(s, "acb", [128, 64], F32)
            P.barrier()
            P.dve(lambda e: e.tensor_tensor(out=lpr[:, 0:64], in0=lam_in[:, 0:64], in1=lam_in[:, 64:128], op=ALU.mult))
            t = P.dve(lambda e: e.tensor_tensor(out=lpr[:, 64:128], in0=lam_in[:, 128:192], in1=lam_in[:, 192:256], op=ALU.mult))
            P.act(lambda e: e.activation(out=lpr[:, 0:64], in_=lpr[:, 0:64], func=AF.Copy, accum_out=lsc[:, 0:1]), deps=[t])
            t = P.act(lambda e: e.activation(out=lpr[:, 64:128], in_=lpr[:, 64:128], func=AF.Copy, accum_out=lsc[:, 1:2]))
            t = P.act(lambda e: e.activation(out=lsc[:, 2:4], in_=lsc[:, 0:2], func=AF.Exp), deps=[t])
            t = P.dve(lambda e: e.tensor_tensor(out=lsc[:, 4:5], in0=lsc[:, 3:4], in1=lsc[:, 2:3], op=ALU.subtract), deps=[t])
            t = P.dve(lambda e: e.tensor_scalar(out=lsc[:, 5:6], in0=lsc[:, 4:5], scalar1=-0.2, scalar2=None, op0=ALU.add), deps=[t])
            t = P.dve(lambda e: e.tensor_scalar(out=lsc[:, 6:7], in0=gsub[:, 0:1], scalar1=0.8, scalar2=None, op0=ALU.mult), deps=[t])
            t_l = t
            neglam = lsc[:, 5:6]
            gs2 = lsc[:, 6:7]
            cvals = {}

            def cbias(c):
                return float(c)
            A.cbias = cbias

            A.osq_free = None
            tl = {}
            tl[0] = load_head(A, l, 0, 0)
            tts = {0: P.sp(lambda e: e.dma_start(out=TS[:, 0], in_=io["bias0"][0].rearrange("v p q -> p v q")), deps=[A.ld_free[0]], dma_sem=ts_sem[0])}
            for h in range(NH):
                sl = h % 2
                if h + 1 < NH:
                    ns = (h + 1) % 2
                    tl[h + 1] = load_head(A, l, h + 1, ns)
                    tts[h + 1] = P.sp(lambda e: e.dma_start(out=TS[:, ns], in_=io["bias0"][h + 1].rearrange("v p q -> p v q")),
                                      deps=[A.ld_free[ns]], dma_sem=ts_sem[ns])
                slope = float(SLOPES[h])
                last_use = None
                for qt in range(4):
                    for m in range(2):
                        (bO, fO), (bD, fD) = od_banks(A)
                        nkb = 16 + 4 * (qt + 1)
                        mmO = None
                        for kb in range(nkb):
                            bS, fS = s_bank(A)
                            mmS = P.pe(lambda e: e.matmul(P.psum[:, bS, :], A.KT[m * 64:(m + 1) * 64, sl, kb * 128:(kb + 1) * 128],
                                                          A.q[m * 64:(m + 1) * 64, sl, qt * 512:(qt + 1) * 512], start=True, stop=True),
                                       deps=[fS, tl[h], tts[h]])
                            if kb < 16:
                                v, c, sc = 0, -slope * (2048 + 512 * qt - 128 * kb), A.pmask[:, 0:1]
                            else:
                                j = (kb - 16) - 4 * qt
                                if j < 0:
                                    v, c, sc = 0, -slope * (512 * qt - 128 * (kb - 16)), 0.0
                                else:
                                    v, c, sc = 1 + j, 0.0, 0.0
                            pi, tp = softmax_block(A, bS, mmS, TS[:, sl, v, :], sc, c)
                            P.pe(lambda e: e.matmul(P.psum[:, bO, :], A.V[:, sl, kb, :], A.pm[:, pi, :], start=(kb == 0), stop=(kb == nkb - 1)),
                                 deps=[tp, fO if kb == 0 else None], sig=False)
                            mmO = P.pe(lambda e: e.matmul(P.psum[:, bD, :], ones_bf[:, :], A.pm[:, pi, :], start=(kb == 0), stop=(kb == nkb - 1)),
                                       deps=[fD if kb == 0 else None])
                            A.pm_free[pi] = mmO
                        last_use = mmO
                        r, trd = recip_den(A, bD, mmO)
                        t1 = P.dve(lambda e: e.tensor_tensor(out=om[:, m, :], in0=P.psum[:, bO, :], in1=A.rd[:, r, :], op=ALU.mult),
                                   deps=[trd, mmO, om_free[m]])
                        P.ps_free[bO] = t1
                        A.rd_free[r] = t1
                    t2 = P.dve(lambda e: e.scalar_tensor_tensor(out=om[:, 0, :], in0=om[:, 1, :], scalar=neglam, in1=om[:, 0, :],
                                                                op0=ALU.mult, op1=ALU.add), deps=[t1, t_l])
                    t3 = P.act(lambda e: e.activation(out=osq[:, :], in_=om[:, 0, :], func=AF.Square), deps=[t2, A.osq_free])
                    bS, fS = s_bank(A)
                    mm = P.pe(lambda e: e.matmul(P.psum[:, bS, :], ones_bf[:, :], osq[:, :], start=True, stop=True), deps=[t3, fS])
                    A.osq_free = mm
                    r = A.rdi
                    A.rdi ^= 1
                    t4 = P.act(lambda e: e.activation(out=A.rd[:, r, :], in_=P.psum[:, bS, :], func=AF.Ln, scale=1.0 / 128, bias=lnb[:, 0:1]),
                               deps=[mm, A.rd_free[r]])
                    P.ps_free[bS] = t4
                    t5 = P.act(lambda e: e.activation(out=A.rd[:, r, :], in_=A.rd[:, r, :], func=AF.Exp, scale=-0.5), deps=[t4])

                    def prod(dst, deps):
                        return P.dve(lambda e: e.scalar_tensor_tensor(out=dst, in0=om[:, 0, :], scalar=gs2, in1=A.rd[:, r, :],
                                                                      op0=ALU.mult, op1=ALU.mult), deps=[t5] + deps)
                    store_mg(A, h, qt * 512, prod)
                    t6 = Tok(P.dve.sem, P.dve.cnt)
                    A.rd_free[r] = t6
                    om_free[0] = t6
                    om_free[1] = t6
                A.ld_free[sl] = last_use
            stage_mem_heads(A, l, kmT, vm, t_kv)
            P.barrier()

    def stage_mem_heads(A, l, kmT, vm, t_kv):
        qk = QKV[l]
        for mh in range(4):
            sl = mh % 2
            tq = P.sp(lambda e: e.dma_start(out=A.q[:, sl, :], in_=qk["qT"][12 + mh]), deps=[A.ld_free[sl]], dma_sem=A.ld_sem[sl])
            last = None
            for qt in range(4):
                (bO, fO), (bD, fD) = od_banks(A)
                for mb in range(2):
                    bS, fS = s_bank(A)
                    mmS = P.pe(lambda e: e.matmul(P.psum[:, bS, :], kmT[:, mh, mb * 128:(mb + 1) * 128], A.q[:, sl, qt * 512:(qt + 1) * 512],
                                                  start=True, stop=True), deps=[fS, tq, t_kv])
                    pi, tp = softmax_block(A, bS, mmS, None, 0.0, 0.0)
                    P.pe(lambda e: e.matmul(P.psum[:, bO, :], vm[:, mb, mh * 128:(mh + 1) * 128], A.pm[:, pi, :], start=(mb == 0), stop=(mb == 1)),
                         deps=[tp, fO if mb == 0 else None], sig=False)
                    mmO = P.pe(lambda e: e.matmul(P.psum[:, bD, :], ones_bf[:, :], A.pm[:, pi, :], start=(mb == 0), stop=(mb == 1)),
                               deps=[fD if mb == 0 else None])
                    A.pm_free[pi] = mmO
                last = mmO
                r, trd = recip_den(A, bD, mmO)

                def prod(dst, deps):
                    return P.dve(lambda e: e.tensor_tensor(out=dst, in0=P.psum[:, bO, :], in1=A.rd[:, r, :], op=ALU.mult), deps=[trd, mmO] + deps)
                store_mg(A, 12 + mh, qt * 512, prod)
                t6 = Tok(P.dve.sem, P.dve.cnt)
                P.ps_free[bO] = t6
                A.rd_free[r] = t6
            A.ld_free[sl] = last

    def stage_ATT1(kmT, vm, t_kv):
        l = 1
        with ExitStack() as s:
            A = attn_common(s)
            A.cbias = lambda c: float(c)
            TS = P.sbuf(s, "bTS", [128, 2, 2, 512], F32)
            ts_sem = [P.new_sem("bts") for _ in range(2)]
            Ob = P.sbuf(s, "bOb", [128, 3, TOK], F32)
            Db = P.sbuf(s, "bDb", [128, 3, TOK], F32)
            mgt = P.sbuf(s, "bmgt", [128, 2, TOK], BF16)
            mgt_sem = [P.new_sem("bmgt") for _ in range(2)]
            mgt_free = [None, None]
            mgi = 0
            ob_free = [None] * 3
            P.barrier()
            order = [(hs, g) for hs in range(4) for g in range(3)]

            def issue_loads(idx):
                hs, g = order[idx]
                h = 4 * g + hs
                sl = idx % 2
                tk = load_head(A, l, h, sl, dil=DIL[g][1])
                tt = P.sp(lambda e: e.dma_start(out=TS[:, sl], in_=io["bias1"][h].rearrange("v p q -> p v q")), deps=[A.ld_free[sl]], dma_sem=ts_sem[sl])
                return [Tok(A.ld_sem[sl], P.dma_cnt[id(A.ld_sem[sl])][1]), tt]
            lt = {0: issue_loads(0)}
            for idx, (hs, g) in enumerate(order):
                h = 4 * g + hs
                sl = idx % 2
                dl = DIL[g][1]
                nU = 16 // dl
                if idx + 1 < len(order):
                    lt[idx + 1] = issue_loads(idx + 1)
                KTv = A.KT[:, sl, :].rearrange("p (U i r) -> p r U i", i=128, r=dl)
                qv = A.q[:, sl, :].rearrange("p (U i r) -> p r U i", i=128, r=dl)
                Ov = Ob[:, g, :].rearrange("p (U i r) -> p r U i", i=128, r=dl)
                Dv = Db[:, g, :].rearrange("p (U i r) -> p r U i", i=128, r=dl)
                last_use = None
                tw = None
                for qg in range(4):
                    if dl == 1:
                        subs = [(0, 4 * qg + st) for st in range(4)]
                    elif dl == 4:
                        subs = [(qg, st) for st in range(4)]
                    else:
                        subs = [(4 * qg + st, 0) for st in range(4)]
                    (bA, fA) = s_bank(A)
                    (bB, fB) = s_bank(A)
                    mmA = mmB = None
                    for st, (r, Uo) in enumerate(subs):
                        Uv = Uo + nU
                        mmA = P.pe(lambda e: e.matmul(P.psum[:, bA, st * 128:(st + 1) * 128], KTv[:, r, Uv - 1, :], qv[:, r, Uo, :], start=True, stop=True),
                                   deps=[fA if st == 0 else None] + lt[idx])
                        mmB = P.pe(lambda e: e.matmul(P.psum[:, bB, st * 128:(st + 1) * 128], KTv[:, r, Uv, :], qv[:, r, Uo, :], start=True, stop=True),
                                   deps=[fB if st == 0 else None])
                    i1 = A.ui
                    A.ui = (i1 + 1) % 3
                    i2 = A.ui
                    A.ui = (i2 + 1) % 3
                    t1 = None
                    for st, (r, Uo) in enumerate(subs):
                        sc = A.pmask[:, 0:1] if Uo == 0 else 0.0
                        t1 = P.dve(lambda e: e.scalar_tensor_tensor(out=A.u[:, i1, st * 128:(st + 1) * 128], in0=P.psum[:, bA, st * 128:(st + 1) * 128],
                                                                    scalar=sc, in1=TS[:, sl, 0, st * 128:(st + 1) * 128], op0=ALU.add, op1=ALU.add),
                                   deps=[mmB, A.u_free[i1]])
                    P.ps_free[bA] = t1
                    t2 = P.dve(lambda e: e.scalar_tensor_tensor(out=A.u[:, i2, :], in0=P.psum[:, bB, :], scalar=0.0, in1=TS[:, sl, 1, :],
                                                                op0=ALU.add, op1=ALU.add), deps=[mmB, A.u_free[i2]])
                    P.ps_free[bB] = t2
                    tA = P.act(lambda e: e.activation(out=A.pm[:, i1, :], in_=A.u[:, i1, :], func=AF.Exp), deps=[t1, A.pm_free[i1]])
                    A.u_free[i1] = tA
                    tB = P.act(lambda e: e.activation(out=A.pm[:, i2, :], in_=A.u[:, i2, :], func=AF.Exp), deps=[t2, A.pm_free[i2]])
                    A.u_free[i2] = tB
                    (bO, fO), (bD, fD) = od_banks(A)
                    mmO = None
                    for st, (r, Uo) in enumerate(subs):
                        Uv = Uo + nU
                        blkA = r * 2 * nU + Uv - 1
                        blkB = r * 2 * nU + Uv
                        osl = slice(st * 128, (st + 1) * 128)
                        P.pe(lambda e: e.matmul(P.psum[:, bO, osl], A.V[:, sl, blkA, :], A.pm[:, i1, osl], start=True, stop=False),
                             deps=[tA, tB, fO if st == 0 else None], sig=False)
                        P.pe(lambda e: e.matmul(P.psum[:, bO, osl], A.V[:, sl, blkB, :], A.pm[:, i2, osl], start=False, stop=True), sig=False)
                        P.pe(lambda e: e.matmul(P.psum[:, bD, osl], ones_bf[:, :], A.pm[:, i1, osl], start=True, stop=False),
                             deps=[fD if st == 0 else None], sig=False)
                        mmO = P.pe(lambda e: e.matmul(P.psum[:, bD, osl], ones_bf[:, :], A.pm[:, i2, osl], start=False, stop=True))
                    A.pm_free[i1] = mmO
                    A.pm_free[i2] = mmO
                    last_use = mmO
                    if dl == 1:
                        od = Ov[:, 0, 4 * qg:4 * qg + 4, :]
                        dd = Dv[:, 0, 4 * qg:4 * qg + 4, :]
                    elif dl == 4:
                        od = Ov[:, qg, 0:4, :]
                        dd = Dv[:, qg, 0:4, :]
                    else:
                        od = Ov[:, 4 * qg:4 * qg + 4, 0, :]
                        dd = Dv[:, 4 * qg:4 * qg + 4, 0, :]
                    srcO = P.psum[:, bO, :].rearrange("p (s i) -> p s i", s=4)
                    srcD = P.psum[:, bD, :].rearrange("p (s i) -> p s i", s=4)
                    P.ps_free[bO] = P.act(lambda e: e.activation(out=od, in_=srcO, func=AF.Copy), deps=[mmO, ob_free[g]])
                    tw = P.dve(lambda e: e.tensor_copy(out=dd, in_=srcD), deps=[mmO, ob_free[g]])
                    P.ps_free[bD] = tw
                A.ld_free[sl] = last_use
                if g == 2:
                    ta = Tok(P.act.sem, P.act.cnt)
                    t = P.dve(lambda e: e.tensor_tensor(out=Db[:, 0, :], in0=Db[:, 0, :], in1=Db[:, 1, :], op=ALU.add), deps=[tw, ta])
                    t = P.dve(lambda e: e.tensor_tensor(out=Db[:, 0, :], in0=Db[:, 0, :], in1=Db[:, 2, :], op=ALU.add), deps=[t])
                    t = P.act(lambda e: e.activation(out=Db[:, 0, :], in_=Db[:, 0, :], func=AF.Ln), deps=[t])
                    t = P.act(lambda e: e.activation(out=Db[:, 0, :], in_=Db[:, 0, :], func=AF.Exp, scale=-1.0), deps=[t])
                    for gg in range(3):
                        mi = mgi
                        mgi ^= 1
                        t7 = P.dve(lambda e: e.tensor_tensor(out=mgt[:, mi, :], in0=Ob[:, gg, :], in1=Db[:, 0, :], op=ALU.mult), deps=[t, mgt_free[mi]])
                        dst = io["mg"][4 * gg + hs]
                        mgt_free[mi] = P.sp(lambda e: e.dma_start(out=dst, in_=mgt[:, mi, :]), deps=[t7], dma_sem=mgt_sem[mi])
                    for gg in range(3):
                        ob_free[gg] = t7
            stage_mem_heads(A, l, kmT, vm, t_kv)
            P.barrier()

    def stage_R(l, x_src, x_dst):
        with ExitStack() as s:
            g_sb, _ = load_const(s, "gR", W["g_mlp", l], [128, NCH])
            ring = make_norm_ring(P, s, lnb)
            actT = P.sbuf(s, "actT", [128, NCH, TT], BF16)
            hid = P.sbuf(s, "hid", [128, 32, TT], BF16)
            NX = 6
            xs = P.sbuf(s, "rxs", [128, NX, 512], F32)
            xs_sem = [P.new_sem("rxl") for _ in range(NX)]
            xst_sem = [P.new_sem("rxs") for _ in range(NX)]
            xs_free = [None] * NX
            xsi = [0]
            rt = P.sbuf(s, "rrt", [128, 3, 512], F32)
            rt_free = [None] * 3
            rti = [0]
            mld_sem = P.new_sem("rml")
            P.barrier()
            xw = {}
            act_free = [None]
            hid_free = [None]
            for t in range(NT):
                t0 = t * TT
                tm = P.sp(lambda e: e.dma_start(out=actT[:], in_=io["mg"][:, :, t0:t0 + TT].rearrange("c p t -> p c t")),
                          deps=[act_free[0]], dma_sem=mld_sem)

                def rmw(src_dram):
                    def evac(cp, o, st, b, mm):
                        oc = cp * 2 + o
                        i = xsi[0]
                        xsi[0] = (i + 1) % NX
                        col = t0 + st * 512
                        ld = P.sp(lambda e: e.dma_start(out=xs[:, i, :], in_=src_dram[oc, :, col:col + 512]),
                                  deps=[xs_free[i], xw.get((oc, t, st))], dma_sem=xs_sem[i])
                        ta = P.dve(lambda e: e.tensor_tensor(out=xs[:, i, :], in0=P.psum[:, b, :], in1=xs[:, i, :], op=ALU.add), deps=[mm, ld])
                        stt = P.sp(lambda e: e.dma_start(out=x_dst[oc, :, col:col + 512], in_=xs[:, i, :]), deps=[ta], dma_sem=xst_sem[i])
                        xs_free[i] = stt
                        xw[(oc, t, st)] = stt
                        return ta
                    return evac
                linear_fm(P, W["w_out", l], [0], range(8), lambda ks, kc, st: (actT[:, kc, st * 512:(st + 1) * 512], tm), TT // 512, rmw(x_src))
                ring[9][0] = Tok(P.pe.sem, P.pe.cnt)
                P.sp.wait_all([v for (k, v) in xw.items() if k[1] == t])
                tn = norm_tile(P, s, x_dst, t0, g_sb, ones_bf, actT, "R", ring=ring)
                for half in range(2):
                    tlast = [None]

                    def evac1(cp, o, st, b, mm):
                        ocl = (cp - half * 16) * 2 + o
                        i = rti[0]
                        rti[0] = (i + 1) % 3
                        t1 = P.act(lambda e: e.activation(out=rt[:, i, :], in_=P.psum[:, b, :], func=AF.Relu), deps=[mm, rt_free[i]])
                        t2 = P.dve(lambda e: e.tensor_tensor(out=hid[:, ocl, st * 512:(st + 1) * 512], in0=rt[:, i, :], in1=rt[:, i, :], op=ALU.mult),
                                   deps=[t1, hid_free[0]])
                        rt_free[i] = t2
                        tlast[0] = t2
                        return t1
                    linear_fm(P, W["w1", l], [0], range(half * 16, half * 16 + 16), lambda ks, kc, st: (actT[:, kc, st * 512:(st + 1) * 512], tn),
                              TT // 512, evac1)
                    th = tlast[0]
                    linear_fm(P, W["w2", l], [half * 4096, half * 4096 + 2048], range(8),
                              lambda ks, kc, st: (hid[:, ks * 16 + kc, st * 512:(st + 1) * 512], th), TT // 512, rmw(x_dst))
                    hid_free[0] = Tok(P.pe.sem, P.pe.cnt)
                act_free[0] = Tok(P.pe.sem, P.pe.cnt)
            P.barrier()

    def stage_final(x_dram):
        with ExitStack() as s:
            g_sb, _ = load_const(s, "gF", io["g_final"], [128, NCH])
            ring = make_norm_ring(P, s, lnb)
            P.barrier()
            for t in range(NT):
                norm_tile(P, s, x_dram, t * TT, g_sb, ones_bf, None, "F", y_dram=io["yT"], ring=ring)
            P.barrier()

    def stage_B(l, x_src, x_dst, memn, tmem):
        with ExitStack() as s:
            kmT = P.sbuf(s, "kmT", [128, 4, 256], BF16)
            vm = P.sbuf(s, "vm", [128, 2, 512], BF16)
            P.barrier()
            t_kv = mem_kv(l, memn, tmem, kmT, vm)
            if l == 0:
                stage_ATT0(kmT, vm, t_kv)
            else:
                stage_ATT1(kmT, vm, t_kv)
        stage_R(l, x_src, x_dst)

    if has["A0"]:
        stage_A(0, io["xT"])
    if has["B0"] or has["B1"]:
        with ExitStack() as sm:
            memn, tmem = mem_norm(sm)
            if has["B0"]:
                stage_B(0, io["xT"], io["x1"], memn, tmem)
                if has["A1"]:
                    stage_A(1, io["x1"])
            if has["B1"]:
                if mode == "B1":
                    stage_B(1, io["xT"], io["x2"], memn, tmem)
                    stage_final(io["x2"])
                else:
                    stage_B(1, io["x1"], io["x1"], memn, tmem)
                    stage_final(io["x1"])

    P.barrier()
    es.close()
    return nc


def _fm(a):
    t, f = a.shape
    return np.ascontiguousarray(a.T.reshape(f // 128, 128, t))


def _gvec(g):
    return np.ascontiguousarray(g.reshape(NCH, 128).T)


_NC_CACHE = {}


def _get_nc(mode):
    if mode not in _NC_CACHE:
        _NC_CACHE[mode] = build(mode)
    return _NC_CACHE[mode]


def _bias_tables():
    ki = np.arange(128, dtype=np.float32)[:, None]
    qi = np.arange(512, dtype=np.float32)[None, :]
    b0 = np.zeros((NH, 5, 128, 512), np.float32)
    b1 = np.zeros((NH, 2, 128, 512), np.float32)
    q1 = (np.arange(512) % 128).astype(np.float32)[None, :]
    for h in range(NH):
        sl = float(SLOPES[h])
        b0[h, 0] = sl * (ki - qi)
        for j in range(4):
            dist = qi - ki - 128.0 * j
            b0[h, 1 + j] = np.where(dist >= 0, -sl * dist, NEG)
        dl = DIL[h // 4][1]
        dprev = 128.0 + q1 - ki
        b1[h, 0] = np.where(dprev <= 128, -sl * dl * dprev, NEG)
        ddiag = q1 - ki
        b1[h, 1] = np.where(ddiag >= 0, -sl * dl * ddiag, NEG)
    return b0, b1


def _run(mode, in_maps):
    nc = _get_nc(mode)
    res = run_bass_kernel_spmd(nc, in_maps, core_ids=list(range(8)))
    return res.results


def kernel(x, mem, g_attn, w_in, w_out, lambda_qk, diff_subln_g, g_mem, w_mem_kv, g_mlp, w_mlp1, w_mlp2, g_final):
    f32 = np.float32
    x = np.asarray(x, f32)
    mem = np.asarray(mem, f32)
    b0, b1 = _bias_tables()
    lamqk = np.ascontiguousarray(np.broadcast_to(np.asarray(lambda_qk, f32).reshape(1, 256), (128, 256)))
    gsub = np.ascontiguousarray(np.asarray(diff_subln_g, f32).reshape(128, 1))
    xT = [_fm(x[c // 2, (c % 2) * TOK:(c % 2 + 1) * TOK]) for c in range(8)]
    memT = [_fm(mem[c // 2]) for c in range(8)]
    pmask = [np.full((128, 1), NEG if c % 2 == 0 else 0.0, f32) for c in range(8)]
    wl = lambda a, l: np.ascontiguousarray(np.asarray(a[l], f32))
    r1 = _run("A0", [{"xT": xT[c], "w_in0": wl(w_in, 0), "g_attn0": _gvec(np.asarray(g_attn[0], f32))} for c in range(8)])

    def prev(r, c, key):
        return r[c - 1][key] if c % 2 == 1 else r[c][key]
    common = lambda c: {"memT": memT[c], "g_mem": _gvec(np.asarray(g_mem, f32)), "prevmask": pmask[c]}
    in2 = []
    for c in range(8):
        d = {"xT": xT[c], "qT0": r1[c]["qT0"], "kT0": r1[c]["kT0"], "vh0": r1[c]["vh0"],
             "kTp0": prev(r1, c, "kT0"), "vhp0": prev(r1, c, "vh0"),
             "w_out0": wl(w_out, 0), "w_kv0": wl(w_mem_kv, 0), "w1_0": wl(w_mlp1, 0), "w2_0": wl(w_mlp2, 0),
             "g_mlp0": _gvec(np.asarray(g_mlp[0], f32)), "lamqk": lamqk, "gsub": gsub, "bias0": b0,
             "w_in1": wl(w_in, 1), "g_attn1": _gvec(np.asarray(g_attn[1], f32))}
        d.update(common(c))
        in2.append(d)
    r2 = _run("B0A1", in2)
    in3 = []
    for c in range(8):
        d = {"xT": r2[c]["x1"], "qT1": r2[c]["qT1"], "kT1": r2[c]["kT1"], "vh1": r2[c]["vh1"],
             "kTp1": prev(r2, c, "kT1"), "vhp1": prev(r2, c, "vh1"),
             "w_out1": wl(w_out, 1), "w_kv1": wl(w_mem_kv, 1), "w1_1": wl(w_mlp1, 1), "w2_1": wl(w_mlp2, 1),
             "g_mlp1": _gvec(np.asarray(g_mlp[1], f32)), "bias1": b1, "g_final": _gvec(np.asarray(g_final, f32))}
        d.update(common(c))
        in3.append(d)
    r3 = _run("B1", in3)
    out = np.empty((4, 4096, D), f32)
    for c in range(8):
        yT = r3[c]["yT"]
        out[c // 2, (c % 2) * TOK:(c % 2 + 1) * TOK] = yT.reshape(D, TOK).T
    return out
```

```python
import math
from contextlib import ExitStack
import numpy as np
import ml_dtypes
import concourse.bass as bass
import concourse.mybir as mybir
from concourse.bass_utils import run_bass_kernel_spmd

F32 = mybir.dt.float32
BF16 = mybir.dt.bfloat16
AF = mybir.ActivationFunctionType
ALU = mybir.AluOpType

D = 2048
NCH = 16
TOK = 2048
TT = 1024
NT = TOK // TT
NH = 12
DFF = 8192
EPS = 1e-6
NEG = -30000.0
DIL = ((128, 1), (512, 4), (2048, 16))


def alibi_slopes(n):
    def pow2(m):
        start = 2.0 ** (-(2.0 ** -(math.log2(m) - 3)))
        return [start * start ** i for i in range(m)]

    def slopes(m):
        if math.log2(m).is_integer():
            return pow2(m)
        c = 2 ** math.floor(math.log2(m))
        return pow2(c) + slopes(2 * c)[0::2][: m - c]
    return np.asarray(sorted(slopes(n), reverse=True), dtype=np.float32)


SLOPES = alibi_slopes(NH)


class Tok:
    __slots__ = ("sem", "val")

    def __init__(self, sem, val):
        self.sem = sem
        self.val = val


class Eng:
    def __init__(self, P, name, attr):
        self.P = P
        self.name = name
        self.attr = attr
        self.sem = P.new_sem("e_" + name)
        self.cnt = 0
        self.waited = {}

    def wait_all(self, deps):
        waits = []
        for d in deps:
            k = id(d.sem)
            if self.waited.get(k, 0) >= d.val:
                continue
            self.waited[k] = d.val
            waits.append(d)

        def body(e):
            for w in waits:
                e.wait_ge(w.sem, w.val)
        if waits:
            getattr(self.P.block, self.attr)(body)

    def __call__(self, fn, deps=(), sig=True, dma_sem=None):
        waits = []
        for d in deps:
            if d is None:
                continue
            k = id(d.sem)
            if self.waited.get(k, 0) >= d.val:
                continue
            self.waited[k] = d.val
            waits.append(d)
        tok = None
        if dma_sem is not None:
            cnt = self.P.dma_cnt.get(id(dma_sem), (dma_sem, 0))[1] + 16
            self.P.dma_cnt[id(dma_sem)] = (dma_sem, cnt)
            tok = Tok(dma_sem, cnt)
        elif sig:
            self.cnt += 1
            tok = Tok(self.sem, self.cnt)

        def body(e):
            for w in waits:
                e.wait_ge(w.sem, w.val)
            ins = fn(e)
            if dma_sem is not None:
                ins.then_inc(dma_sem, 16)
            elif sig:
                ins.then_inc(self.sem, 1)
        getattr(self.P.block, self.attr)(body)
        return tok


class Prog:
    def __init__(self, nc, es, block):
        self.nc = nc
        self.es = es
        self.block = block
        self.dma_cnt = {}
        self.nsem = 0
        self.pe = Eng(self, "pe", "tensor")
        self.act = Eng(self, "act", "scalar")
        self.dve = Eng(self, "dve", "vector")
        self.pool = Eng(self, "pool", "gpsimd")
        self.sp = Eng(self, "sp", "sync")
        self.engs = [self.pe, self.act, self.dve, self.pool, self.sp]
        self.psum = es.enter_context(nc.psum_tensor("psum", [128, 8, 512], F32))
        self.ps_free = [None] * 8
        self.ps_i = 0
        self.NSLOT = 3
        self.wp = es.enter_context(nc.sbuf_tensor("wpanel", [128, self.NSLOT, 16, 256], BF16))
        self.wp_sem = [self.new_sem("wp%d" % i) for i in range(self.NSLOT)]
        self.wp_free = [None] * self.NSLOT
        self.wp_i = 0
        self.evq = 0

    def new_sem(self, name):
        self.nsem += 1
        return self.es.enter_context(self.nc.semaphore("%s_%d" % (name, self.nsem)))

    def sbuf(self, es, name, shape, dt):
        self.nsem += 1
        return es.enter_context(self.nc.sbuf_tensor("sb_%s_%d" % (name, self.nsem), shape, dt))

    def bank(self):
        i = self.ps_i
        self.ps_i = (i + 1) % 8
        return i, self.ps_free[i]

    def barrier(self):
        toks = [Tok(e.sem, e.cnt) for e in self.engs if e.cnt > 0]
        toks += [Tok(sem, cnt) for (sem, cnt) in self.dma_cnt.values()]
        for e in self.engs:
            e.wait_all(toks)
        self.ps_free = [None] * 8
        self.wp_free = [None] * self.NSLOT

    def load_panel(self, w2d, k0, c0, extra_deps=()):
        s = self.wp_i
        self.wp_i = (s + 1) % self.NSLOT
        src = w2d[k0:k0 + 2048, c0:c0 + 256].rearrange("(kc p) n -> p kc n", p=128)
        dst = self.wp[:, s]
        tok = self.pool(lambda e: e.dma_start(out=dst, in_=src), deps=[self.wp_free[s], *extra_deps],
                        dma_sem=self.wp_sem[s])
        return s, tok

    def ev_engine(self):
        self.evq += 1
        return self.act if (self.evq & 1) else self.dve


def norm_tile(P, es, x_dram, t0, g_sb, ones_bf, out_h, name, y_dram=None, ring=None):
    xs, xs_sem, xs_free, sq, sq_free, rstd, lnb = ring[:7]
    banks = []
    for st in range(TT // 512):
        banks.append(P.bank())
    last_mm = None
    for c in range(NCH):
        s = c % 2
        src = x_dram[c, :, t0:t0 + TT]
        ld = P.sp(lambda e, s=s, src=src: e.dma_start(out=xs[:, s], in_=src), deps=[xs_free[s]], dma_sem=xs_sem[s])
        t_sq = P.act(lambda e, s=s: e.activation(out=sq[:, s], in_=xs[:, s], func=AF.Square), deps=[ld, sq_free[s]])
        xs_free[s] = t_sq
        for st in range(TT // 512):
            b, bfree = banks[st]
            last_mm = P.pe(lambda e, b=b, s=s, st=st, c=c: e.matmul(P.psum[:, b, :], ones_bf[:, :], sq[:, s, st * 512:(st + 1) * 512],
                                                                     start=(c == 0), stop=(c == NCH - 1)),
                           deps=[t_sq, bfree if c == 0 else None], sig=(st == TT // 512 - 1))
        sq_free[s] = last_mm
    for st in range(TT // 512):
        b, _ = banks[st]
        t1 = P.act(lambda e, b=b, st=st: e.activation(out=rstd[:, st * 512:(st + 1) * 512], in_=P.psum[:, b, :], func=AF.Ln,
                                                      scale=1.0 / D, bias=lnb[:, 0:1]), deps=[last_mm, rstd_free(ring)])
        P.ps_free[b] = t1
        t2 = P.act(lambda e, st=st: e.activation(out=rstd[:, st * 512:(st + 1) * 512], in_=rstd[:, st * 512:(st + 1) * 512],
                                                 func=AF.Exp, scale=-0.5), deps=[t1])
    t_r = t2
    last = None
    for c in range(NCH):
        s = c % 2
        src = x_dram[c, :, t0:t0 + TT]
        ld = P.sp(lambda e, s=s, src=src: e.dma_start(out=xs[:, s], in_=src), deps=[xs_free[s]], dma_sem=xs_sem[s])
        if y_dram is None:
            t = P.dve(lambda e, s=s, c=c: e.scalar_tensor_tensor(out=out_h[:, c, :], in0=xs[:, s], scalar=g_sb[:, c:c + 1], in1=rstd[:, :],
                                                                 op0=ALU.mult, op1=ALU.mult), deps=[ld, t_r, ring_hfree(ring)])
            xs_free[s] = t
            last = t
        else:
            t = P.dve(lambda e, s=s, c=c: e.scalar_tensor_tensor(out=xs[:, s], in0=xs[:, s], scalar=g_sb[:, c:c + 1], in1=rstd[:, :],
                                                                 op0=ALU.mult, op1=ALU.mult), deps=[ld, t_r])
            dst = y_dram[c, :, t0:t0 + TT]
            st_tok = P.sp(lambda e, s=s, dst=dst: e.dma_start(out=dst, in_=xs[:, s]), deps=[t], dma_sem=ring[7][s])
            xs_free[s] = st_tok
            last = st_tok
    ring[8][0] = last
    return last


def rstd_free(ring):
    return ring[8][0]


def ring_hfree(ring):
    return ring[9][0]


def make_norm_ring(P, es, lnb):
    xs = P.sbuf(es, "nxs", [128, 2, TT], F32)
    sq = P.sbuf(es, "nsq", [128, 2, TT], BF16)
    rstd = P.sbuf(es, "nrstd", [128, TT], F32)
    xs_sem = [P.new_sem("nxs") for _ in range(2)]
    st_sem = [P.new_sem("nst") for _ in range(2)]
    return [xs, xs_sem, [None, None], sq, [None, None], rstd, lnb, st_sem, [None], [None]]


def linear_fm(P, w2d, ksegs, colpanels, rhs_fn, nst, evac_fn, w_deps=()):
    for cp in colpanels:
        banks = {}
        for ks, k0 in enumerate(ksegs):
            slot, ltok = P.load_panel(w2d, k0, cp * 256, extra_deps=w_deps)
            last = None
            for o in range(2):
                for st in range(nst):
                    if ks == 0:
                        banks[(o, st)] = P.bank()
                    b, bfree = banks[(o, st)]
                    for kc in range(16):
                        rhs, rdep = rhs_fn(ks, kc, st)
                        first = (ks == 0 and kc == 0)
                        lastmm = (ks == len(ksegs) - 1 and kc == 15)
                        last = P.pe(lambda e, b=b, slot=slot, kc=kc, o=o, rhs=rhs, first=first, lastmm=lastmm:
                                    e.matmul(P.psum[:, b, :], P.wp[:, slot, kc, o * 128:(o + 1) * 128], rhs, start=first, stop=lastmm),
                                    deps=[ltok, rdep, bfree if first else None], sig=(kc == 15))
                    if ks == len(ksegs) - 1:
                        P.ps_free[b] = evac_fn(cp, o, st, b, last)
            P.wp_free[slot] = last


def build(mode):
    nc = bass.Bass("TRN2", target_bir_lowering=False)
    es = ExitStack()

    def dram(name, shape, dt, kind):
        if kind == "in":
            return nc.dram_tensor(name, shape, dt, kind="ExternalInput").ap()
        if kind == "out":
            return nc.dram_tensor(name, shape, dt, kind="ExternalOutput").ap()
        return nc.dram_tensor(name, shape, dt).ap()

    fused = (mode == "FUSED")
    has = {"A0": mode in ("A0", "FUSED"), "B0": mode in ("B0A1", "FUSED"), "A1": mode in ("B0A1", "FUSED"),
           "B1": mode in ("B1", "FUSED")}
    io = {}
    io["xT"] = dram("xT", [NCH, 128, TOK], F32, "in")
    layers = [l for l in (0, 1) if has["A%d" % l] or has["B%d" % l]]
    W = {}
    for l in (0, 1):
        if has["A%d" % l]:
            W["w_in", l] = dram("w_in%d" % l, [D, 5120], F32, "in")
            W["g_attn", l] = dram("g_attn%d" % l, [128, NCH], F32, "in")
        if has["B%d" % l]:
            W["w_out", l] = dram("w_out%d" % l, [D, D], F32, "in")
            W["w_kv", l] = dram("w_kv%d" % l, [D, 1024], F32, "in")
            W["w1", l] = dram("w1_%d" % l, [D, DFF], F32, "in")
            W["w2", l] = dram("w2_%d" % l, [DFF, D], F32, "in")
            W["g_mlp", l] = dram("g_mlp%d" % l, [128, NCH], F32, "in")
    if has["B0"] or has["B1"]:
        io["memT"] = dram("memT", [NCH, 128, 256], F32, "in")
        io["g_mem"] = dram("g_mem", [128, NCH], F32, "in")
        io["prevmask"] = dram("prevmask", [128, 1], F32, "in")
    if has["B0"]:
        io["lamqk"] = dram("lamqk", [128, 256], F32, "in")
        io["gsub"] = dram("gsub", [128, 1], F32, "in")
        io["bias0"] = dram("bias0", [NH, 5, 128, 512], F32, "in")
    if has["B1"]:
        io["bias1"] = dram("bias1", [NH, 2, 128, 512], F32, "in")
        io["g_final"] = dram("g_final", [128, NCH], F32, "in")
        io["yT"] = dram("yT", [NCH, 128, TOK], F32, "out")

    def qkv(l, kind):
        return dict(qT=dram("qT%d" % l, [16, 128, TOK], BF16, kind), kT=dram("kT%d" % l, [NH, 128, TOK], BF16, kind),
                    vh=dram("vh%d" % l, [NH, TOK, 128], BF16, kind))
    QKV = {}
    PREV = {}
    for l in (0, 1):
        a, b = has["A%d" % l], has["B%d" % l]
        if a and b:
            QKV[l] = qkv(l, "scratch")
        elif a:
            QKV[l] = qkv(l, "out")
        elif b:
            QKV[l] = qkv(l, "in")
        if b:
            if fused:
                PREV[l] = None
            else:
                PREV[l] = dict(kT=dram("kTp%d" % l, [NH, 128, TOK], BF16, "in"), vh=dram("vhp%d" % l, [NH, TOK, 128], BF16, "in"))
    if has["B0"] or has["B1"]:
        io["mg"] = dram("mg", [16, 128, TOK], BF16, "scratch" if fused else "out")
    if has["B0"]:
        io["x1"] = dram("x1", [NCH, 128, TOK], F32, "scratch" if fused else "out")
    if mode == "B1":
        io["x2"] = dram("x2", [NCH, 128, TOK], F32, "scratch")

    block = es.enter_context(nc.Block())
    P = Prog(nc, es, block)

    ones_bf = P.sbuf(es, "ones_bf", [128, 128], BF16)
    lnb = P.sbuf(es, "lnb", [128, 1], F32)
    t_c1 = P.pool(lambda e: e.memset(ones_bf[:], 1.0))
    t_c2 = P.pool(lambda e: e.memset(lnb[:], EPS))
    csem = P.new_sem("const")

    def load_const(es_, name, src, shape, dt=F32):
        t = P.sbuf(es_, name, shape, dt)
        tok = P.sp(lambda e: e.dma_start(out=t[:], in_=src), dma_sem=csem)
        return t, tok

    P.barrier()

    def stage_A(l, x_dram):
        with ExitStack() as s:
            g_sb, tg = load_const(s, "gA", W["g_attn", l], [128, NCH])
            ring = make_norm_ring(P, s, lnb)
            hT = P.sbuf(s, "hT", [128, NCH, TT], BF16)
            stg = P.sbuf(s, "stgA", [128, 2, 2, TT], BF16)
            stg_sem = [P.new_sem("stgA") for _ in range(2)]
            stg_free = [None, None]
            vst = P.sbuf(s, "vstA", [128, 2, TT // 128, 256], BF16)
            vst_sem = [P.new_sem("vstA") for _ in range(2)]
            vst_free = [None, None]
            P.barrier()
            q = QKV[l]
            sc_mix = (64 ** -0.5) if l == 0 else (128 ** -0.5)
            sc_mem = 128 ** -0.5
            h_readers = [None]
            for t in range(NT):
                t0 = t * TT
                ring[9][0] = h_readers[0]
                tn = norm_tile(P, s, x_dram, t0, g_sb, ones_bf, hT, "A", ring=ring)
                cnt = [0]
                for cp in list(range(0, 12)) + [18, 19]:
                    si = cnt[0] % 2
                    cnt[0] += 1
                    evs = []

                    def evac(cp_, o, st, b, mm, si=si, evs=evs):
                        if cp_ < 6:
                            sc = sc_mix
                        elif cp_ < 12:
                            sc = 1.0
                        else:
                            sc = sc_mem
                        eng = P.ev_engine()
                        dst = stg[:, si, o, st * 512:(st + 1) * 512]
                        if eng is P.act:
                            tk = eng(lambda e: e.activation(out=dst, in_=P.psum[:, b, :], func=AF.Copy, scale=float(sc)),
                                     deps=[mm, stg_free[si]])
                        else:
                            tk = eng(lambda e: e.tensor_scalar(out=dst, in0=P.psum[:, b, :], scalar1=float(sc), scalar2=None, op0=ALU.mult),
                                     deps=[mm, stg_free[si]])
                        evs.append(tk)
                        return tk
                    linear_fm(P, W["w_in", l], [0], [cp], lambda ks, kc, st: (hT[:, kc, st * 512:(st + 1) * 512], tn), TT // 512, evac)
                    last = None
                    for o in range(2):
                        if cp < 6:
                            dst = q["qT"][cp * 2 + o, :, t0:t0 + TT]
                        elif cp < 12:
                            dst = q["kT"][(cp - 6) * 2 + o, :, t0:t0 + TT]
                        else:
                            dst = q["qT"][12 + (cp - 18) * 2 + o, :, t0:t0 + TT]
                        last = P.sp(lambda e, dst=dst, o=o, si=si: e.dma_start(out=dst, in_=stg[:, si, o, :]), deps=evs, dma_sem=stg_sem[si])
                    stg_free[si] = last
                for cp in range(12, 18):
                    si = cnt[0] % 2
                    cnt[0] += 1
                    slot, ltok = P.load_panel(W["w_in", l], 0, cp * 256)
                    evs = []
                    last = None
                    for bp in range(TT // 256):
                        b, bfree = P.bank()
                        for j in range(2):
                            blk = bp * 2 + j
                            for kc in range(16):
                                last = P.pe(lambda e, b=b, j=j, blk=blk, kc=kc, slot=slot:
                                            e.matmul(P.psum[:, b, j * 256:(j + 1) * 256], hT[:, kc, blk * 128:(blk + 1) * 128], P.wp[:, slot, kc, :],
                                                     start=(kc == 0), stop=(kc == 15)),
                                            deps=[ltok, tn, bfree if (kc == 0 and j == 0) else None], sig=(kc == 15))
                        eng = P.ev_engine()
                        dst = vst[:, si, bp * 2:bp * 2 + 2, :]
                        srcp = P.psum[:, b, :].rearrange("p (j c) -> p j c", j=2)
                        if eng is P.act:
                            tk = eng(lambda e, dst=dst, srcp=srcp: e.activation(out=dst, in_=srcp, func=AF.Copy), deps=[last, vst_free[si]])
                        else:
                            tk = eng(lambda e, dst=dst, srcp=srcp: e.tensor_copy(out=dst, in_=srcp), deps=[last, vst_free[si]])
                        P.ps_free[b] = tk
                        evs.append(tk)
                    P.wp_free[slot] = last
                    lst = None
                    for o in range(2):
                        hh = (cp - 12) * 2 + o
                        dst = q["vh"][hh, t0:t0 + TT, :].rearrange("(blk p) d -> p blk d", p=128)
                        lst = P.sp(lambda e, dst=dst, o=o, si=si: e.dma_start(out=dst, in_=vst[:, si, :, o * 128:(o + 1) * 128]),
                                   deps=evs, dma_sem=vst_sem[si])
                    vst_free[si] = lst
                    h_readers[0] = last
            P.barrier()


    def mem_norm(s):
        memn = P.sbuf(s, "memn", [128, NCH, 256], BF16)
        s = ExitStack()
        gm, _ = load_const(s, "gmem", io["g_mem"], [128, NCH])
        mraw = P.sbuf(s, "mraw", [128, NCH, 256], F32)
        msq = P.sbuf(s, "msq", [128, NCH, 256], BF16)
        mrs = P.sbuf(s, "mrs", [128, 256], F32)
        P.sp(lambda e: e.dma_start(out=mraw[:], in_=io["memT"].rearrange("c p m -> p c m")), dma_sem=csem)
        P.barrier()
        t1 = P.act(lambda e: e.activation(out=msq[:], in_=mraw[:], func=AF.Square))
        b, bf = P.bank()
        for c in range(NCH):
            mm = P.pe(lambda e: e.matmul(P.psum[:, b, 0:256], ones_bf[:, :], msq[:, c, :], start=(c == 0), stop=(c == NCH - 1)),
                      deps=[t1, bf], sig=(c == NCH - 1))
        t2 = P.act(lambda e: e.activation(out=mrs[:], in_=P.psum[:, b, 0:256], func=AF.Ln, scale=1.0 / D, bias=lnb[:, 0:1]), deps=[mm])
        P.ps_free[b] = t2
        t3 = P.act(lambda e: e.activation(out=mrs[:], in_=mrs[:], func=AF.Exp, scale=-0.5), deps=[t2])
        for c in range(NCH):
            t4 = P.dve(lambda e: e.scalar_tensor_tensor(out=memn[:, c, :], in0=mraw[:, c, :], scalar=gm[:, c:c + 1], in1=mrs[:, :],
                                                        op0=ALU.mult, op1=ALU.mult), deps=[t3])
        P.barrier()
        s.close()
        return memn, t4

    def mem_kv(l, memn, tmem, kmT, vm):
        last = None
        for cp in range(4):
            slot, ltok = P.load_panel(W["w_kv", l], 0, cp * 256)
            if cp < 2:
                for o in range(2):
                    b, bf = P.bank()
                    for kc in range(16):
                        mm = P.pe(lambda e: e.matmul(P.psum[:, b, 0:256], P.wp[:, slot, kc, o * 128:(o + 1) * 128], memn[:, kc, :],
                                                     start=(kc == 0), stop=(kc == 15)), deps=[ltok, tmem, bf if kc == 0 else None], sig=(kc == 15))
                    last = P.dve(lambda e: e.tensor_copy(out=kmT[:, cp * 2 + o, :], in_=P.psum[:, b, 0:256]), deps=[mm])
                    P.ps_free[b] = last
            else:
                for mb in range(2):
                    b, bf = P.bank()
                    for kc in range(16):
                        mm = P.pe(lambda e: e.matmul(P.psum[:, b, 0:256], memn[:, kc, mb * 128:(mb + 1) * 128], P.wp[:, slot, kc, :],
                                                     start=(kc == 0), stop=(kc == 15)), deps=[ltok, tmem, bf if kc == 0 else None], sig=(kc == 15))
                    last = P.dve(lambda e: e.tensor_copy(out=vm[:, mb, (cp - 2) * 256:(cp - 1) * 256], in_=P.psum[:, b, 0:256]), deps=[mm])
                    P.ps_free[b] = last
            P.wp_free[slot] = mm
        return last

    class AttnCtx:
        pass

    def attn_common(s):
        A = AttnCtx()
        A.KT = P.sbuf(s, "aKT", [128, 2, 2 * TOK], BF16)
        A.V = P.sbuf(s, "aV", [128, 2, 32, 128], BF16)
        A.q = P.sbuf(s, "aq", [128, 2, TOK], BF16)
        A.ld_sem = [P.new_sem("ald") for _ in range(2)]
        A.ld_free = [None, None]
        A.u = P.sbuf(s, "au", [128, 4, 512], F32)
        A.u_free = [None] * 4
        A.pm = P.sbuf(s, "apm", [128, 4, 512], BF16)
        A.pm_free = [None] * 4
        A.ui = 0
        A.rd = P.sbuf(s, "ard", [128, 2, 512], F32)
        A.rd_free = [None, None]
        A.rdi = 0
        A.stg = P.sbuf(s, "astg", [128, 2, 512], BF16)
        A.stg_sem = [P.new_sem("astg") for _ in range(2)]
        A.stg_free = [None, None]
        A.stgi = 0
        A.pmask, _ = load_const(s, "pmask", io["prevmask"], [128, 1])
        A.sb = 0
        A.ob = 0
        return A

    def s_bank(A):
        b = A.sb
        A.sb = (b + 1) % 4
        return b, P.ps_free[b]

    def od_banks(A):
        p = A.ob
        A.ob ^= 1
        return (4 + 2 * p, P.ps_free[4 + 2 * p]), (5 + 2 * p, P.ps_free[5 + 2 * p])

    def softmax_block(A, bS, mmS, bias_ap, scalar, c):
        i = A.ui
        A.ui = (i + 1) % 4
        if bias_ap is not None:
            t1 = P.dve(lambda e: e.scalar_tensor_tensor(out=A.u[:, i, :], in0=P.psum[:, bS, :], scalar=scalar, in1=bias_ap,
                                                        op0=ALU.add, op1=ALU.add), deps=[mmS, A.u_free[i]])
            P.ps_free[bS] = t1
            t2 = P.act(lambda e: e.activation(out=A.pm[:, i, :], in_=A.u[:, i, :], func=AF.Exp, bias=A.cbias(c)), deps=[t1, A.pm_free[i]])
            A.u_free[i] = t2
        else:
            t2 = P.act(lambda e: e.activation(out=A.pm[:, i, :], in_=P.psum[:, bS, :], func=AF.Exp), deps=[mmS, A.pm_free[i]])
            P.ps_free[bS] = t2
        return i, t2

    def recip_den(A, bD, mm_last):
        r = A.rdi
        A.rdi ^= 1
        t1 = P.act(lambda e: e.activation(out=A.rd[:, r, :], in_=P.psum[:, bD, :], func=AF.Ln), deps=[mm_last, A.rd_free[r]])
        P.ps_free[bD] = t1
        t2 = P.act(lambda e: e.activation(out=A.rd[:, r, :], in_=A.rd[:, r, :], func=AF.Exp, scale=-1.0), deps=[t1])
        return r, t2

    def store_mg(A, h, col0, producer):
        si = A.stgi
        A.stgi ^= 1
        tk = producer(A.stg[:, si, :], [A.stg_free[si]])
        dst = io["mg"][h, :, col0:col0 + 512]
        A.stg_free[si] = P.sp(lambda e: e.dma_start(out=dst, in_=A.stg[:, si, :]), deps=[tk], dma_sem=A.stg_sem[si])

    def load_head(A, l, h, slot, dil=None):
        qk = QKV[l]
        pv = PREV[l]
        fr = [A.ld_free[slot]]
        sem = A.ld_sem[slot]
        P.sp(lambda e: e.dma_start(out=A.KT[:, slot, 0:TOK], in_=pv["kT"][h]), deps=fr, dma_sem=sem)
        P.sp(lambda e: e.dma_start(out=A.KT[:, slot, TOK:2 * TOK], in_=qk["kT"][h]), deps=fr, dma_sem=sem)
        tk = None
        if dil is None or dil == 1:
            P.sp(lambda e: e.dma_start(out=A.V[:, slot, 0:16, :], in_=pv["vh"][h].rearrange("(b p) d -> p b d", p=128)), deps=fr, dma_sem=sem)
            P.sp(lambda e: e.dma_start(out=A.V[:, slot, 16:32, :], in_=qk["vh"][h].rearrange("(b p) d -> p b d", p=128)), deps=fr, dma_sem=sem)
        else:
            nU = 16 // dil
            for r in range(dil):
                for half, src in ((0, pv["vh"][h]), (1, qk["vh"][h])):
                    sv = src.rearrange("(U i r) d -> r i U d", i=128, r=dil)[r]
                    b0 = r * 2 * nU + half * nU
                    P.sp(lambda e: e.dma_start(out=A.V[:, slot, b0:b0 + nU, :], in_=sv), deps=fr, dma_sem=sem)
        tk = P.sp(lambda e: e.dma_start(out=A.q[:, slot, :], in_=qk["qT"][h]), deps=fr, dma_sem=sem)
        return tk

    def stage_ATT0(kmT, vm, t_kv):
        l = 0
        with ExitStack() as s:
            A = attn_common(s)
            TS = P.sbuf(s, "aTS", [128, 2, 5, 512], F32)
            ts_sem = [P.new_sem("ats") for _ in range(2)]
            lam_in, _ = load_const(s, "lamin", io["lamqk"], [128, 256])
            gsub, _ = load_const(s, "gsub", io["gsub"], [128, 1])
            lsc = P.sbuf(s, "lsc", [128, 8], F32)
            lpr = P.sbuf(s, "lpr", [128, 128], F32)
            om = P.sbuf(s, "aom", [128, 2, 512], F32)
            om_free = [None, None]
            osq = P.sbuf(s, "aosq", [128, 512], BF16)
            cb = P.sbuf(s, "acb", [128, 64], F32)
            P.barrier()
            P.dve(lambda e: e.tensor_tensor(out=lpr[:, 0:64], in0=lam_in[:, 0:64], in1=lam_in[:, 64:128], op=ALU.mult))
            t = P.dve(lambda e: e.tensor_tensor(out=lpr[:, 64:128], in0=lam_in[:, 128:192], in1=lam_in[:, 192:256], op=ALU.mult))
            P.act(lambda e: e.activation(out=lpr[:, 0:64], in_=lpr[:, 0:64], func=AF.Copy, accum_out=lsc[:, 0:1]), deps=[t])
            t = P.act(lambda e: e.activation(out=lpr[:, 64:128], in_=lpr[:, 64:128], func=AF.Copy, accum_out=lsc[:, 1:2]))
            t = P.act(lambda e: e.activation(out=lsc[:, 2:4], in_=lsc[:, 0:2], func=AF.Exp), deps=[t])
            t = P.dve(lambda e: e.tensor_tensor(out=lsc[:, 4:5], in0=lsc[:, 3:4], in1=lsc[:, 2:3], op=ALU.subtract), deps=[t])
            t = P.dve(lambda e: e.tensor_scalar(out=lsc[:, 5:6], in0=lsc[:, 4:5], scalar1=-0.2, scalar2=None, op0=ALU.add), deps=[t])
            t = P.dve(lambda e: e.tensor_scalar(out=lsc[:, 6:7], in0=gsub[:, 0:1], scalar1=0.8, scalar2=None, op0=ALU.mult), deps=[t])
            t_l = t
            neglam = lsc[:, 5:6]
            gs2 = lsc[:, 6:7]
            cvals = {}

            def cbias(c):
                return float(c)
            A.cbias = cbias

            A.osq_free = None
            tl = {}
            tl[0] = load_head(A, l, 0, 0)
            tts = {0: P.sp(lambda e: e.dma_start(out=TS[:, 0], in_=io["bias0"][0].rearrange("v p q -> p v q")), deps=[A.ld_free[0]], dma_sem=ts_sem[0])}
            for h in range(NH):
                sl = h % 2
                if h + 1 < NH:
                    ns = (h + 1) % 2
                    tl[h + 1] = load_head(A, l, h + 1, ns)
                    tts[h + 1] = P.sp(lambda e: e.dma_start(out=TS[:, ns], in_=io["bias0"][h + 1].rearrange("v p q -> p v q")),
                                      deps=[A.ld_free[ns]], dma_sem=ts_sem[ns])
                slope = float(SLOPES[h])
                cur = {}
                pend = []
                deferred = []
                last_use = [None]

                def issue_S(qt, m, kb, nkb):
                    bS, fS = s_bank(A)
                    mmS = P.pe(lambda e: e.matmul(P.psum[:, bS, :], A.KT[m * 64:(m + 1) * 64, sl, kb * 128:(kb + 1) * 128],
                                                  A.q[m * 64:(m + 1) * 64, sl, qt * 512:(qt + 1) * 512], start=True, stop=True),
                               deps=[fS, tl[h], tts[h]])
                    if kb < 16:
                        v, c, sc = 0, -slope * (2048 + 512 * qt - 128 * kb), A.pmask[:, 0:1]
                    else:
                        j = (kb - 16) - 4 * qt
                        if j < 0:
                            v, c, sc = 0, -slope * (512 * qt - 128 * (kb - 16)), 0.0
                        else:
                            v, c, sc = 1 + j, 0.0, 0.0
                    pi, tp = softmax_block(A, bS, mmS, TS[:, sl, v, :], sc, c)
                    return (qt, m, kb, nkb, pi, tp)

                def tail(qt, t1):
                    t2 = P.dve(lambda e: e.scalar_tensor_tensor(out=om[:, 0, :], in0=om[:, 1, :], scalar=neglam, in1=om[:, 0, :],
                                                                op0=ALU.mult, op1=ALU.add), deps=[t1, t_l])
                    t3 = P.act(lambda e: e.activation(out=osq[:, :], in_=om[:, 0, :], func=AF.Square), deps=[t2, A.osq_free])

                    def later():
                        bS, fS = s_bank(A)
                        mm = P.pe(lambda e: e.matmul(P.psum[:, bS, :], ones_bf[:, :], osq[:, :], start=True, stop=True), deps=[t3, fS])
                        A.osq_free = mm
                        r = A.rdi
                        A.rdi ^= 1
                        t4 = P.act(lambda e: e.activation(out=A.rd[:, r, :], in_=P.psum[:, bS, :], func=AF.Ln, scale=1.0 / 128, bias=lnb[:, 0:1]),
                                   deps=[mm, A.rd_free[r]])
                        P.ps_free[bS] = t4
                        t5 = P.act(lambda e: e.activation(out=A.rd[:, r, :], in_=A.rd[:, r, :], func=AF.Exp, scale=-0.5), deps=[t4])

                        def prod(dst, deps):
                            return P.dve(lambda e: e.scalar_tensor_tensor(out=dst, in0=om[:, 0, :], scalar=gs2, in1=A.rd[:, r, :],
                                                                          op0=ALU.mult, op1=ALU.mult), deps=[t5] + deps)
                        store_mg(A, h, qt * 512, prod)
                        t6 = Tok(P.dve.sem, P.dve.cnt)
                        A.rd_free[r] = t6
                        om_free[0] = t6
                        om_free[1] = t6
                    deferred.append([6, later])

                def issue_PV(qt, m, kb, nkb, pi, tp):
                    if kb == 0:
                        cur["b"] = od_banks(A)
                    (bO, fO), (bD, fD) = cur["b"]
                    P.pe(lambda e: e.matmul(P.psum[:, bO, :], A.V[:, sl, kb, :], A.pm[:, pi, :], start=(kb == 0), stop=(kb == nkb - 1)),
                         deps=[tp, fO if kb == 0 else None], sig=False)
                    mmO = P.pe(lambda e: e.matmul(P.psum[:, bD, :], ones_bf[:, :], A.pm[:, pi, :], start=(kb == 0), stop=(kb == nkb - 1)),
                               deps=[fD if kb == 0 else None])
                    A.pm_free[pi] = mmO
                    last_use[0] = mmO
                    if kb == nkb - 1:
                        r, trd = recip_den(A, bD, mmO)
                        t1 = P.dve(lambda e: e.tensor_tensor(out=om[:, m, :], in0=P.psum[:, bO, :], in1=A.rd[:, r, :], op=ALU.mult),
                                   deps=[trd, mmO, om_free[m]])
                        P.ps_free[bO] = t1
                        A.rd_free[r] = t1
                        if m == 1:
                            tail(qt, t1)
                    for d in list(deferred):
                        d[0] -= 1
                        if d[0] <= 0:
                            deferred.remove(d)
                            d[1]()

                PD = 2
                for qt in range(4):
                    for m in range(2):
                        nkb = 16 + 4 * (qt + 1)
                        for kb in range(nkb):
                            pend.append(issue_S(qt, m, kb, nkb))
                            if len(pend) > PD:
                                issue_PV(*pend.pop(0))
                while pend:
                    issue_PV(*pend.pop(0))
                for d in list(deferred):
                    d[1]()
                last_use = last_use[0]
                A.ld_free[sl] = last_use
            stage_mem_heads(A, l, kmT, vm, t_kv)
            P.barrier()

    def stage_mem_heads(A, l, kmT, vm, t_kv):
        qk = QKV[l]
        for mh in range(4):
            sl = mh % 2
            tq = P.sp(lambda e: e.dma_start(out=A.q[:, sl, :], in_=qk["qT"][12 + mh]), deps=[A.ld_free[sl]], dma_sem=A.ld_sem[sl])
            last = None
            for qt in range(4):
                (bO, fO), (bD, fD) = od_banks(A)
                for mb in range(2):
                    bS, fS = s_bank(A)
                    mmS = P.pe(lambda e: e.matmul(P.psum[:, bS, :], kmT[:, mh, mb * 128:(mb + 1) * 128], A.q[:, sl, qt * 512:(qt + 1) * 512],
                                                  start=True, stop=True), deps=[fS, tq, t_kv])
                    pi, tp = softmax_block(A, bS, mmS, None, 0.0, 0.0)
                    P.pe(lambda e: e.matmul(P.psum[:, bO, :], vm[:, mb, mh * 128:(mh + 1) * 128], A.pm[:, pi, :], start=(mb == 0), stop=(mb == 1)),
                         deps=[tp, fO if mb == 0 else None], sig=False)
                    mmO = P.pe(lambda e: e.matmul(P.psum[:, bD, :], ones_bf[:, :], A.pm[:, pi, :], start=(mb == 0), stop=(mb == 1)),
                               deps=[fD if mb == 0 else None])
                    A.pm_free[pi] = mmO
                last = mmO
                r, trd = recip_den(A, bD, mmO)

                def prod(dst, deps):
                    return P.dve(lambda e: e.tensor_tensor(out=dst, in0=P.psum[:, bO, :], in1=A.rd[:, r, :], op=ALU.mult), deps=[trd, mmO] + deps)
                store_mg(A, 12 + mh, qt * 512, prod)
                t6 = Tok(P.dve.sem, P.dve.cnt)
                P.ps_free[bO] = t6
                A.rd_free[r] = t6
            A.ld_free[sl] = last

    def stage_ATT1(kmT, vm, t_kv):
        l = 1
        with ExitStack() as s:
            A = attn_common(s)
            A.cbias = lambda c: float(c)
            TS = P.sbuf(s, "bTS", [128, 2, 2, 512], F32)
            ts_sem = [P.new_sem("bts") for _ in range(2)]
            Ob = P.sbuf(s, "bOb", [128, 3, TOK], F32)
            Db = P.sbuf(s, "bDb", [128, 3, TOK], F32)
            mgt = P.sbuf(s, "bmgt", [128, 2, TOK], BF16)
            mgt_sem = [P.new_sem("bmgt") for _ in range(2)]
            mgt_free = [None, None]
            mgi = 0
            ob_free = [None] * 3
            P.barrier()
            order = [(hs, g) for hs in range(4) for g in range(3)]

            def issue_loads(idx):
                hs, g = order[idx]
                h = 4 * g + hs
                sl = idx % 2
                tk = load_head(A, l, h, sl, dil=DIL[g][1])
                tt = P.sp(lambda e: e.dma_start(out=TS[:, sl], in_=io["bias1"][h].rearrange("v p q -> p v q")), deps=[A.ld_free[sl]], dma_sem=ts_sem[sl])
                return [Tok(A.ld_sem[sl], P.dma_cnt[id(A.ld_sem[sl])][1]), tt]
            lt = {0: issue_loads(0)}
            for idx, (hs, g) in enumerate(order):
                h = 4 * g + hs
                sl = idx % 2
                dl = DIL[g][1]
                nU = 16 // dl
                if idx + 1 < len(order):
                    lt[idx + 1] = issue_loads(idx + 1)
                KTv = A.KT[:, sl, :].rearrange("p (U i r) -> p r U i", i=128, r=dl)
                qv = A.q[:, sl, :].rearrange("p (U i r) -> p r U i", i=128, r=dl)
                Ov = Ob[:, g, :].rearrange("p (U i r) -> p r U i", i=128, r=dl)
                Dv = Db[:, g, :].rearrange("p (U i r) -> p r U i", i=128, r=dl)
                last_use = None
                tw = None
                for qg in range(4):
                    if dl == 1:
                        subs = [(0, 4 * qg + st) for st in range(4)]
                    elif dl == 4:
                        subs = [(qg, st) for st in range(4)]
                    else:
                        subs = [(4 * qg + st, 0) for st in range(4)]
                    (bA, fA) = s_bank(A)
                    (bB, fB) = s_bank(A)
                    mmA = mmB = None
                    for st, (r, Uo) in enumerate(subs):
                        Uv = Uo + nU
                        mmA = P.pe(lambda e: e.matmul(P.psum[:, bA, st * 128:(st + 1) * 128], KTv[:, r, Uv - 1, :], qv[:, r, Uo, :], start=True, stop=True),
                                   deps=[fA if st == 0 else None] + lt[idx])
                        mmB = P.pe(lambda e: e.matmul(P.psum[:, bB, st * 128:(st + 1) * 128], KTv[:, r, Uv, :], qv[:, r, Uo, :], start=True, stop=True),
                                   deps=[fB if st == 0 else None])
                    i1 = A.ui
                    A.ui = (i1 + 1) % 4
                    i2 = A.ui
                    A.ui = (i2 + 1) % 4
                    t1 = None
                    for st, (r, Uo) in enumerate(subs):
                        sc = A.pmask[:, 0:1] if Uo == 0 else 0.0
                        t1 = P.dve(lambda e: e.scalar_tensor_tensor(out=A.u[:, i1, st * 128:(st + 1) * 128], in0=P.psum[:, bA, st * 128:(st + 1) * 128],
                                                                    scalar=sc, in1=TS[:, sl, 0, st * 128:(st + 1) * 128], op0=ALU.add, op1=ALU.add),
                                   deps=[mmB, A.u_free[i1]])
                    P.ps_free[bA] = t1
                    t2 = P.dve(lambda e: e.scalar_tensor_tensor(out=A.u[:, i2, :], in0=P.psum[:, bB, :], scalar=0.0, in1=TS[:, sl, 1, :],
                                                                op0=ALU.add, op1=ALU.add), deps=[mmB, A.u_free[i2]])
                    P.ps_free[bB] = t2
                    tA = P.act(lambda e: e.activation(out=A.pm[:, i1, :], in_=A.u[:, i1, :], func=AF.Exp), deps=[t1, A.pm_free[i1]])
                    A.u_free[i1] = tA
                    tB = P.act(lambda e: e.activation(out=A.pm[:, i2, :], in_=A.u[:, i2, :], func=AF.Exp), deps=[t2, A.pm_free[i2]])
                    A.u_free[i2] = tB
                    (bO, fO), (bD, fD) = od_banks(A)
                    mmO = None
                    for st, (r, Uo) in enumerate(subs):
                        Uv = Uo + nU
                        blkA = r * 2 * nU + Uv - 1
                        blkB = r * 2 * nU + Uv
                        osl = slice(st * 128, (st + 1) * 128)
                        P.pe(lambda e: e.matmul(P.psum[:, bO, osl], A.V[:, sl, blkA, :], A.pm[:, i1, osl], start=True, stop=False),
                             deps=[tA, tB, fO if st == 0 else None], sig=False)
                        P.pe(lambda e: e.matmul(P.psum[:, bO, osl], A.V[:, sl, blkB, :], A.pm[:, i2, osl], start=False, stop=True), sig=False)
                        P.pe(lambda e: e.matmul(P.psum[:, bD, osl], ones_bf[:, :], A.pm[:, i1, osl], start=True, stop=False),
                             deps=[fD if st == 0 else None], sig=False)
                        mmO = P.pe(lambda e: e.matmul(P.psum[:, bD, osl], ones_bf[:, :], A.pm[:, i2, osl], start=False, stop=True))
                    A.pm_free[i1] = mmO
                    A.pm_free[i2] = mmO
                    last_use = mmO
                    if dl == 1:
                        od = Ov[:, 0, 4 * qg:4 * qg + 4, :]
                        dd = Dv[:, 0, 4 * qg:4 * qg + 4, :]
                    elif dl == 4:
                        od = Ov[:, qg, 0:4, :]
                        dd = Dv[:, qg, 0:4, :]
                    else:
                        od = Ov[:, 4 * qg:4 * qg + 4, 0, :]
                        dd = Dv[:, 4 * qg:4 * qg + 4, 0, :]
                    srcO = P.psum[:, bO, :].rearrange("p (s i) -> p s i", s=4)
                    srcD = P.psum[:, bD, :].rearrange("p (s i) -> p s i", s=4)
                    P.ps_free[bO] = P.act(lambda e: e.activation(out=od, in_=srcO, func=AF.Copy), deps=[mmO, ob_free[g]])
                    tw = P.dve(lambda e: e.tensor_copy(out=dd, in_=srcD), deps=[mmO, ob_free[g]])
                    P.ps_free[bD] = tw
                A.ld_free[sl] = last_use
                if g == 2:
                    ta = Tok(P.act.sem, P.act.cnt)
                    t = P.dve(lambda e: e.tensor_tensor(out=Db[:, 0, :], in0=Db[:, 0, :], in1=Db[:, 1, :], op=ALU.add), deps=[tw, ta])
                    t = P.dve(lambda e: e.tensor_tensor(out=Db[:, 0, :], in0=Db[:, 0, :], in1=Db[:, 2, :], op=ALU.add), deps=[t])
                    t = P.act(lambda e: e.activation(out=Db[:, 0, :], in_=Db[:, 0, :], func=AF.Ln), deps=[t])
                    t = P.act(lambda e: e.activation(out=Db[:, 0, :], in_=Db[:, 0, :], func=AF.Exp, scale=-1.0), deps=[t])
                    for gg in range(3):
                        mi = mgi
                        mgi ^= 1
                        t7 = P.dve(lambda e: e.tensor_tensor(out=mgt[:, mi, :], in0=Ob[:, gg, :], in1=Db[:, 0, :], op=ALU.mult), deps=[t, mgt_free[mi]])
                        dst = io["mg"][4 * gg + hs]
                        mgt_free[mi] = P.sp(lambda e: e.dma_start(out=dst, in_=mgt[:, mi, :]), deps=[t7], dma_sem=mgt_sem[mi])
                    for gg in range(3):
                        ob_free[gg] = t7
            stage_mem_heads(A, l, kmT, vm, t_kv)
            P.barrier()

    def stage_R(l, x_src, x_dst):
        with ExitStack() as s:
            g_sb, _ = load_const(s, "gR", W["g_mlp", l], [128, NCH])
            ring = make_norm_ring(P, s, lnb)
            actT = P.sbuf(s, "actT", [128, NCH, TT], BF16)
            hid = P.sbuf(s, "hid", [128, 32, TT], BF16)
            NX = 6
            xs = P.sbuf(s, "rxs", [128, NX, 512], F32)
            xs_sem = [P.new_sem("rxl") for _ in range(NX)]
            xst_sem = [P.new_sem("rxs") for _ in range(NX)]
            xs_free = [None] * NX
            xsi = [0]
            rt = P.sbuf(s, "rrt", [128, 3, 512], F32)
            rt_free = [None] * 3
            rti = [0]
            mld_sem = P.new_sem("rml")
            P.barrier()
            xw = {}
            act_free = [None]
            hid_free = [None]
            for t in range(NT):
                t0 = t * TT
                tm = P.sp(lambda e: e.dma_start(out=actT[:], in_=io["mg"][:, :, t0:t0 + TT].rearrange("c p t -> p c t")),
                          deps=[act_free[0]], dma_sem=mld_sem)

                def rmw(src_dram):
                    def evac(cp, o, st, b, mm):
                        oc = cp * 2 + o
                        i = xsi[0]
                        xsi[0] = (i + 1) % NX
                        col = t0 + st * 512
                        ld = P.sp(lambda e: e.dma_start(out=xs[:, i, :], in_=src_dram[oc, :, col:col + 512]),
                                  deps=[xs_free[i], xw.get((oc, t, st))], dma_sem=xs_sem[i])
                        ta = P.dve(lambda e: e.tensor_tensor(out=xs[:, i, :], in0=P.psum[:, b, :], in1=xs[:, i, :], op=ALU.add), deps=[mm, ld])
                        stt = P.sp(lambda e: e.dma_start(out=x_dst[oc, :, col:col + 512], in_=xs[:, i, :]), deps=[ta], dma_sem=xst_sem[i])
                        xs_free[i] = stt
                        xw[(oc, t, st)] = stt
                        return ta
                    return evac
                linear_fm(P, W["w_out", l], [0], range(8), lambda ks, kc, st: (actT[:, kc, st * 512:(st + 1) * 512], tm), TT // 512, rmw(x_src))
                ring[9][0] = Tok(P.pe.sem, P.pe.cnt)
                P.sp.wait_all([v for (k, v) in xw.items() if k[1] == t])
                tn = norm_tile(P, s, x_dst, t0, g_sb, ones_bf, actT, "R", ring=ring)
                for half in range(2):
                    tlast = [None]

                    def evac1(cp, o, st, b, mm):
                        ocl = (cp - half * 16) * 2 + o
                        i = rti[0]
                        rti[0] = (i + 1) % 3
                        t1 = P.act(lambda e: e.activation(out=rt[:, i, :], in_=P.psum[:, b, :], func=AF.Relu), deps=[mm, rt_free[i]])
                        t2 = P.dve(lambda e: e.tensor_tensor(out=hid[:, ocl, st * 512:(st + 1) * 512], in0=rt[:, i, :], in1=rt[:, i, :], op=ALU.mult),
                                   deps=[t1, hid_free[0]])
                        rt_free[i] = t2
                        tlast[0] = t2
                        return t1
                    linear_fm(P, W["w1", l], [0], range(half * 16, half * 16 + 16), lambda ks, kc, st: (actT[:, kc, st * 512:(st + 1) * 512], tn),
                              TT // 512, evac1)
                    th = tlast[0]
                    linear_fm(P, W["w2", l], [half * 4096, half * 4096 + 2048], range(8),
                              lambda ks, kc, st: (hid[:, ks * 16 + kc, st * 512:(st + 1) * 512], th), TT // 512, rmw(x_dst))
                    hid_free[0] = Tok(P.pe.sem, P.pe.cnt)
                act_free[0] = Tok(P.pe.sem, P.pe.cnt)
            P.barrier()

    def stage_final(x_dram):
        with ExitStack() as s:
            g_sb, _ = load_const(s, "gF", io["g_final"], [128, NCH])
            ring = make_norm_ring(P, s, lnb)
            P.barrier()
            for t in range(NT):
                norm_tile(P, s, x_dram, t * TT, g_sb, ones_bf, None, "F", y_dram=io["yT"], ring=ring)
            P.barrier()

    def stage_B(l, x_src, x_dst, memn, tmem):
        with ExitStack() as s:
            kmT = P.sbuf(s, "kmT", [128, 4, 256], BF16)
            vm = P.sbuf(s, "vm", [128, 2, 512], BF16)
            P.barrier()
            t_kv = mem_kv(l, memn, tmem, kmT, vm)
            if l == 0:
                stage_ATT0(kmT, vm, t_kv)
            else:
                stage_ATT1(kmT, vm, t_kv)
        stage_R(l, x_src, x_dst)

    if has["A0"]:
        stage_A(0, io["xT"])
    if has["B0"] or has["B1"]:
        with ExitStack() as sm:
            memn, tmem = mem_norm(sm)
            if has["B0"]:
                stage_B(0, io["xT"], io["x1"], memn, tmem)
                if has["A1"]:
                    stage_A(1, io["x1"])
            if has["B1"]:
                if mode == "B1":
                    stage_B(1, io["xT"], io["x2"], memn, tmem)
                    stage_final(io["x2"])
                else:
                    stage_B(1, io["x1"], io["x1"], memn, tmem)
                    stage_final(io["x1"])

    P.barrier()
    es.close()
    return nc


def _fm(a):
    t, f = a.shape
    return np.ascontiguousarray(a.T.reshape(f // 128, 128, t))


def _gvec(g):
    return np.ascontiguousarray(g.reshape(NCH, 128).T)


_NC_CACHE = {}


def _get_nc(mode):
    if mode not in _NC_CACHE:
        _NC_CACHE[mode] = build(mode)
    return _NC_CACHE[mode]


def _bias_tables():
    ki = np.arange(128, dtype=np.float32)[:, None]
    qi = np.arange(512, dtype=np.float32)[None, :]
    b0 = np.zeros((NH, 5, 128, 512), np.float32)
    b1 = np.zeros((NH, 2, 128, 512), np.float32)
    q1 = (np.arange(512) % 128).astype(np.float32)[None, :]
    for h in range(NH):
        sl = float(SLOPES[h])
        b0[h, 0] = sl * (ki - qi)
        for j in range(4):
            dist = qi - ki - 128.0 * j
            b0[h, 1 + j] = np.where(dist >= 0, -sl * dist, NEG)
        dl = DIL[h // 4][1]
        dprev = 128.0 + q1 - ki
        b1[h, 0] = np.where(dprev <= 128, -sl * dl * dprev, NEG)
        ddiag = q1 - ki
        b1[h, 1] = np.where(ddiag >= 0, -sl * dl * ddiag, NEG)
    return b0, b1


def _run(mode, in_maps):
    nc = _get_nc(mode)
    res = run_bass_kernel_spmd(nc, in_maps, core_ids=list(range(8)))
    return res.results


def kernel(x, mem, g_attn, w_in, w_out, lambda_qk, diff_subln_g, g_mem, w_mem_kv, g_mlp, w_mlp1, w_mlp2, g_final):
    f32 = np.float32
    x = np.asarray(x, f32)
    mem = np.asarray(mem, f32)
    b0, b1 = _bias_tables()
    lamqk = np.ascontiguousarray(np.broadcast_to(np.asarray(lambda_qk, f32).reshape(1, 256), (128, 256)))
    gsub = np.ascontiguousarray(np.asarray(diff_subln_g, f32).reshape(128, 1))
    xT = [_fm(x[c // 2, (c % 2) * TOK:(c % 2 + 1) * TOK]) for c in range(8)]
    memT = [_fm(mem[c // 2]) for c in range(8)]
    pmask = [np.full((128, 1), NEG if c % 2 == 0 else 0.0, f32) for c in range(8)]
    wl = lambda a, l: np.ascontiguousarray(np.asarray(a[l], f32))
    r1 = _run("A0", [{"xT": xT[c], "w_in0": wl(w_in, 0), "g_attn0": _gvec(np.asarray(g_attn[0], f32))} for c in range(8)])

    def prev(r, c, key):
        return r[c - 1][key] if c % 2 == 1 else r[c][key]
    common = lambda c: {"memT": memT[c], "g_mem": _gvec(np.asarray(g_mem, f32)), "prevmask": pmask[c]}
    in2 = []
    for c in range(8):
        d = {"xT": xT[c], "qT0": r1[c]["qT0"], "kT0": r1[c]["kT0"], "vh0": r1[c]["vh0"],
             "kTp0": prev(r1, c, "kT0"), "vhp0": prev(r1, c, "vh0"),
             "w_out0": wl(w_out, 0), "w_kv0": wl(w_mem_kv, 0), "w1_0": wl(w_mlp1, 0), "w2_0": wl(w_mlp2, 0),
             "g_mlp0": _gvec(np.asarray(g_mlp[0], f32)), "lamqk": lamqk, "gsub": gsub, "bias0": b0,
             "w_in1": wl(w_in, 1), "g_attn1": _gvec(np.asarray(g_attn[1], f32))}
        d.update(common(c))
        in2.append(d)
    r2 = _run("B0A1", in2)
    in3 = []
    for c in range(8):
        d = {"xT": r2[c]["x1"], "qT1": r2[c]["qT1"], "kT1": r2[c]["kT1"], "vh1": r2[c]["vh1"],
             "kTp1": prev(r2, c, "kT1"), "vhp1": prev(r2, c, "vh1"),
             "w_out1": wl(w_out, 1), "w_kv1": wl(w_mem_kv, 1), "w1_1": wl(w_mlp1, 1), "w2_1": wl(w_mlp2, 1),
             "g_mlp1": _gvec(np.asarray(g_mlp[1], f32)), "bias1": b1, "g_final": _gvec(np.asarray(g_final, f32))}
        d.update(common(c))
        in3.append(d)
    r3 = _run("B1", in3)
    out = np.empty((4, 4096, D), f32)
    for c in range(8):
        yT = r3[c]["yT"]
        out[c // 2, (c % 2) * TOK:(c % 2 + 1) * TOK] = yT.reshape(D, TOK).T
    return out
```
